# Optimizing a Trainium2 kernel written in Bass

```python
import jax, jax.numpy as jnp
from jax import lax
import numpy as np

D_MODEL = 2048
BATCH = 4
SEQ = 4096
DEPTH = 4

N_MIXERS = 3
CHUNK = 128
EPS = 1e-6

RET_HEADS = 8
RET_DK = D_MODEL // RET_HEADS
RET_DV = 2 * RET_DK
ROPE_BASE = 10000.0

RWKV_HEAD = 64
RWKV_HEADS = D_MODEL // RWKV_HEAD
RWKV_DECAY_LORA = 96
RWKV_A_LORA = 96
RWKV_GATE_LORA = 256
RWKV_IN = 3 * D_MODEL + RWKV_DECAY_LORA + RWKV_A_LORA + RWKV_GATE_LORA
RWKV_LN_EPS = 64e-5

MLSTM_HEADS = 4
MLSTM_DQK = D_MODEL // (2 * MLSTM_HEADS)
MLSTM_DV = D_MODEL // MLSTM_HEADS
MLSTM_CONV = 4
MLSTM_IN = 2 * MLSTM_HEADS * MLSTM_DQK + 2 * D_MODEL + 2 * MLSTM_HEADS
GATE_CAP = 15.0

D_FF = 5632
FFN_CONV = 3

N_RET = (DEPTH + 2) // 3
N_RWKV = (DEPTH + 1) // 3
N_MLSTM = DEPTH // 3

kernel_name = 'hybrid_retention_rwkv7_mlstm_trunk'


def rms_norm(x, g):
    x32 = x.astype(jnp.float32)
    y = x32 * lax.rsqrt(jnp.mean(x32 * x32, axis=-1, keepdims=True) + EPS)
    return (y * g.astype(jnp.float32)).astype(x.dtype)


def head_norm(x, eps, center):
    x = x.astype(jnp.float32)
    if center:
        x = x - jnp.mean(x, axis=-1, keepdims=True)
    return x * lax.rsqrt(jnp.mean(x * x, axis=-1, keepdims=True) + eps)


def modulate(h, shift, scale):
    return h * (1.0 + scale[:, None, :]) + shift[:, None, :]


def rotary(x, pos):
    half = x.shape[-1] // 2
    inv = ROPE_BASE ** (-jnp.arange(half, dtype=jnp.float32) / half)
    ang = pos.astype(jnp.float32)[:, None] * inv[None, :]
    cos = jnp.cos(ang)[None, :, None, :]
    sin = jnp.sin(ang)[None, :, None, :]
    x1, x2 = x[..., :half], x[..., half:]
    return jnp.concatenate([x1 * cos - x2 * sin, x1 * sin + x2 * cos], axis=-1)


def causal_dwconv(x, w, b):
    k_width, ch = w.shape
    y = lax.conv_general_dilated(x, w.reshape(k_width, 1, ch).astype(x.dtype), window_strides=(1,), padding=[(k_width - 1, 0)], dimension_numbers=('NWC', 'WIO', 'NWC'), feature_group_count=ch)
    return y + b


def token_shift(z):
    return jnp.pad(z[:, :-1], ((0, 0), (1, 0), (0, 0)))


def to_chunks(z):
    b, t = z.shape[:2]
    z = z.reshape((b, t // CHUNK, CHUNK) + z.shape[2:])
    return jnp.moveaxis(jnp.moveaxis(z, 1, 0), 2, 3)


def from_chunks(z):
    z = jnp.moveaxis(jnp.moveaxis(z, 3, 2), 0, 1)
    return z.reshape((z.shape[0], z.shape[1] * z.shape[2]) + z.shape[3:])


def retention_chunked(q, k, v, log_gamma):
    n_b, _, n_h, d_k = q.shape
    idx = jnp.arange(CHUNK, dtype=jnp.float32)
    diff = idx[:, None] - idx[None, :]
    causal = diff >= 0
    decay_intra = jnp.where(causal, jnp.exp(log_gamma[:, None, None] * jnp.where(causal, diff, 0.0)), 0.0)
    decay_q = jnp.exp(log_gamma[:, None] * (idx + 1.0))[..., None]
    decay_k = jnp.exp(log_gamma[:, None] * (CHUNK - 1.0 - idx))[..., None]
    decay_chunk = jnp.exp(log_gamma * CHUNK)[:, None, None]

    def step(state, inp):
        qc, kc, vc = inp
        scores = jnp.einsum('bhtd,bhsd->bhts', qc, kc) * decay_intra
        out = jnp.einsum('bhts,bhse->bhte', scores, vc) + jnp.einsum('bhtd,bhde->bhte', qc * decay_q, state)
        state = decay_chunk * state + jnp.einsum('bhsd,bhse->bhde', kc * decay_k, vc)
        return state, out

    s0 = jnp.zeros((n_b, n_h, d_k, v.shape[-1]), jnp.float32)
    _, out = lax.scan(step, s0, (to_chunks(q), to_chunks(k), to_chunks(v)))
    return from_chunks(out)


def mlstm_chunked(q, k, v, logi, logf):
    n_b, _, n_h, d_qk = q.shape
    d_v = v.shape[-1]
    tril = jnp.tril(jnp.ones((CHUNK, CHUNK), dtype=bool))

    def step(carry, inp):
        c_st, n_st, m_st = carry
        qc, kc, vc, li, lf = inp
        b = jnp.cumsum(lf, axis=-1)
        dmat = jnp.where(tril, b[..., :, None] - b[..., None, :] + li[..., None, :], -jnp.inf)
        m_inter = b + m_st[..., None]
        m_t = jnp.maximum(m_inter, jnp.max(dmat, axis=-1))
        scores = jnp.einsum('bhtd,bhsd->bhts', qc, kc) * jnp.exp(dmat - m_t[..., None])
        w_inter = jnp.exp(m_inter - m_t)
        num = jnp.einsum('bhts,bhse->bhte', scores, vc) + w_inter[..., None] * jnp.einsum('bhtd,bhed->bhte', qc, c_st)
        den = jnp.sum(scores, axis=-1) + w_inter * jnp.einsum('bhtd,bhd->bht', qc, n_st)
        h = num / jnp.maximum(jnp.abs(den), jnp.exp(-m_t))[..., None]
        b_last = b[..., -1]
        g = b_last[..., None] - b + li
        m_new = jnp.maximum(b_last + m_st, jnp.max(g, axis=-1))
        wk = jnp.exp(g - m_new[..., None])
        carry_scale = jnp.exp(b_last + m_st - m_new)
        c_st = carry_scale[..., None, None] * c_st + jnp.einsum('bhse,bhsd->bhed', vc * wk[..., None], kc)
        n_st = carry_scale[..., None] * n_st + jnp.einsum('bhs,bhsd->bhd', wk, kc)
        return (c_st, n_st, m_new), h

    init = (jnp.zeros((n_b, n_h, d_v, d_qk), jnp.float32), jnp.zeros((n_b, n_h, d_qk), jnp.float32), jnp.zeros((n_b, n_h), jnp.float32))
    _, h = lax.scan(step, init, (to_chunks(q), to_chunks(k), to_chunks(v), to_chunks(logi), to_chunks(logf)))
    return from_chunks(h)


def rwkv7_scan(r, w, k, v, a, b):
    n_b, _, n_h, n = r.shape

    def step(state, inp):
        r_t, w_t, k_t, v_t, a_t, b_t = inp
        sa = jnp.einsum('bhvk,bhk->bhv', state, a_t)
        state = state * w_t[:, :, None, :] + sa[..., :, None] * b_t[..., None, :] + v_t[..., :, None] * k_t[..., None, :]
        return state, jnp.einsum('bhvk,bhk->bhv', state, r_t)

    xs = tuple(jnp.moveaxis(z, 1, 0) for z in (r, w, k, v, a, b))
    _, y = lax.scan(step, jnp.zeros((n_b, n_h, n, n), jnp.float32), xs)
    return jnp.moveaxis(y, 0, 1)


def retention_mixer(h, w_in, w_out):
    n_b, t_len, _ = h.shape
    qk_w = RET_HEADS * RET_DK
    q, k, v, g = jnp.split(h @ w_in, [qk_w, 2 * qk_w, 2 * qk_w + RET_HEADS * RET_DV], axis=-1)
    pos = jnp.arange(t_len)
    q = rotary(q.reshape(n_b, t_len, RET_HEADS, RET_DK).astype(jnp.float32), pos)
    k = rotary(k.reshape(n_b, t_len, RET_HEADS, RET_DK).astype(jnp.float32), pos) * (RET_DK ** -0.5)
    v = v.reshape(n_b, t_len, RET_HEADS, RET_DV).astype(jnp.float32)
    log_gamma = jnp.log1p(-jnp.power(2.0, -5.0 - jnp.arange(RET_HEADS, dtype=jnp.float32)))
    o = head_norm(retention_chunked(q, k, v, log_gamma), EPS, True)
    o = o.reshape(n_b, t_len, RET_HEADS * RET_DV).astype(h.dtype) * jax.nn.silu(g)
    return o @ w_out


def rwkv7_mixer(h, w_in, mu, w0, w2, a0, a2, g2, k_k, k_a, r_k, ln_w, ln_b, w_out):
    n_b, t_len, d = h.shape
    p = h @ w_in
    p = p + mu * (token_shift(p) - p)
    r, k, v, w_lo, a_lo, g_lo = jnp.split(p, [d, 2 * d, 3 * d, 3 * d + RWKV_DECAY_LORA, 3 * d + RWKV_DECAY_LORA + RWKV_A_LORA], axis=-1)
    w = -jax.nn.softplus(-(w0 + jnp.tanh(w_lo) @ w2)) - 0.5
    decay = jnp.exp(-jnp.exp(w.astype(jnp.float32)))
    a = jax.nn.sigmoid(a0 + a_lo @ a2)
    g = jax.nn.sigmoid(g_lo) @ g2

    def heads(z):
        return z.reshape(n_b, t_len, RWKV_HEADS, RWKV_HEAD).astype(jnp.float32)

    kk = heads(k * k_k)
    kk = kk / jnp.maximum(jnp.sqrt(jnp.sum(kk * kk, axis=-1, keepdims=True)), 1e-12)
    k = k * (1.0 + (a - 1.0) * k_a)
    r_h, k_h, v_h, a_h = heads(r), heads(k), heads(v), heads(a)
    y = rwkv7_scan(r_h, heads(decay), k_h, v_h, -kk, kk * a_h)
    y = head_norm(y, RWKV_LN_EPS, True) * ln_w.reshape(RWKV_HEADS, RWKV_HEAD) + ln_b.reshape(RWKV_HEADS, RWKV_HEAD)
    y = y + jnp.sum(r_h * k_h * r_k, axis=-1, keepdims=True) * v_h
    y = y.reshape(n_b, t_len, d).astype(h.dtype) * g
    return y @ w_out


def softcap(z):
    return GATE_CAP * jnp.tanh(z / GATE_CAP)


def mlstm_mixer(h, w_in, conv_w, conv_b, gate_b, norm_g, w_out):
    n_b, t_len, d = h.shape
    qk_w = 2 * MLSTM_HEADS * MLSTM_DQK
    qk, v, o, gates = jnp.split(h @ w_in, [qk_w, qk_w + d, qk_w + 2 * d], axis=-1)
    qk = jax.nn.silu(causal_dwconv(qk, conv_w, conv_b))
    q, k = jnp.split(qk, 2, axis=-1)
    q = q.reshape(n_b, t_len, MLSTM_HEADS, MLSTM_DQK).astype(jnp.float32)
    k = k.reshape(n_b, t_len, MLSTM_HEADS, MLSTM_DQK).astype(jnp.float32) * (MLSTM_DQK ** -0.5)
    v = v.reshape(n_b, t_len, MLSTM_HEADS, MLSTM_DV).astype(jnp.float32)
    i_pre, f_pre = jnp.split(gates.astype(jnp.float32) + gate_b.astype(jnp.float32), 2, axis=-1)
    logi = softcap(i_pre)
    logf = jax.nn.log_sigmoid(softcap(f_pre))
    hh = head_norm(mlstm_chunked(q, k, v, logi, logf), EPS, False) * norm_g.astype(jnp.float32).reshape(MLSTM_HEADS, MLSTM_DV)
    hh = hh.reshape(n_b, t_len, d).astype(h.dtype) * jax.nn.sigmoid(o)
    return hh @ w_out


def conv_glu_ffn(h, w_up, conv_w, conv_b, w_down):
    gate, val = jnp.split(h @ w_up, 2, axis=-1)
    gate = causal_dwconv(gate, conv_w, conv_b)
    return (jax.nn.silu(gate) * val) @ w_down


def setup_inputs(seed: int = 0) -> dict:
    key = jax.random.key(seed)
    ks = iter(jax.random.split(key, 48))
    D = D_MODEL

    def nrm(shape, scale):
        return scale * jax.random.normal(next(ks), shape, jnp.float32)

    def unif(shape, lo, hi):
        return jax.random.uniform(next(ks), shape, jnp.float32, lo, hi)

    f_bias = jnp.broadcast_to(jnp.linspace(3.0, 6.0, MLSTM_HEADS, dtype=jnp.float32), (N_MLSTM, MLSTM_HEADS))
    gate_b = jnp.concatenate([nrm((N_MLSTM, MLSTM_HEADS), 0.1), f_bias + nrm((N_MLSTM, MLSTM_HEADS), 0.1)], axis=-1)
    return {
        'x': nrm((BATCH, SEQ, D), 1.0),
        'c': nrm((BATCH, D), 1.0),
        'mod_w': nrm((DEPTH, D, 6 * D), 0.5 * D ** -0.5),
        'mod_b': nrm((DEPTH, 6 * D), 0.02),
        'norm_mix_g': 1.0 + nrm((DEPTH, D), 0.02),
        'norm_ffn_g': 1.0 + nrm((DEPTH, D), 0.02),
        'ret_w_in': nrm((N_RET, D, 2 * RET_HEADS * RET_DK + 2 * RET_HEADS * RET_DV), D ** -0.5),
        'ret_w_out': nrm((N_RET, RET_HEADS * RET_DV, D), (RET_HEADS * RET_DV) ** -0.5),
        'rwkv_w_in': nrm((N_RWKV, D, RWKV_IN), D ** -0.5),
        'rwkv_mu': unif((N_RWKV, RWKV_IN), 0.0, 1.0),
        'rwkv_w0': unif((N_RWKV, D), -6.0, -1.0),
        'rwkv_w2': nrm((N_RWKV, RWKV_DECAY_LORA, D), 0.5 * RWKV_DECAY_LORA ** -0.5),
        'rwkv_a0': nrm((N_RWKV, D), 0.1),
        'rwkv_a2': nrm((N_RWKV, RWKV_A_LORA, D), 0.5 * RWKV_A_LORA ** -0.5),
        'rwkv_g2': nrm((N_RWKV, RWKV_GATE_LORA, D), RWKV_GATE_LORA ** -0.5),
        'rwkv_k_k': 0.85 + nrm((N_RWKV, D), 0.05),
        'rwkv_k_a': 1.0 + nrm((N_RWKV, D), 0.05),
        'rwkv_r_k': nrm((N_RWKV, RWKV_HEADS, RWKV_HEAD), 0.1),
        'rwkv_ln_w': 1.0 + nrm((N_RWKV, D), 0.02),
        'rwkv_ln_b': nrm((N_RWKV, D), 0.02),
        'rwkv_w_out': nrm((N_RWKV, D, D), D ** -0.5),
        'mlstm_w_in': nrm((N_MLSTM, D, MLSTM_IN), D ** -0.5),
        'mlstm_conv_w': nrm((N_MLSTM, MLSTM_CONV, 2 * MLSTM_HEADS * MLSTM_DQK), MLSTM_CONV ** -0.5),
        'mlstm_conv_b': nrm((N_MLSTM, 2 * MLSTM_HEADS * MLSTM_DQK), 0.02),
        'mlstm_gate_b': gate_b,
        'mlstm_norm_g': 1.0 + nrm((N_MLSTM, D), 0.02),
        'mlstm_w_out': nrm((N_MLSTM, D, D), D ** -0.5),
        'ffn_w_up': nrm((DEPTH, D, 2 * D_FF), D ** -0.5),
        'ffn_conv_w': nrm((DEPTH, FFN_CONV, D_FF), FFN_CONV ** -0.5),
        'ffn_conv_b': nrm((DEPTH, D_FF), 0.02),
        'ffn_w_down': nrm((DEPTH, D_FF, D), D_FF ** -0.5),
        'final_g': 1.0 + nrm((D,), 0.02),
        'final_mod_w': nrm((D, 2 * D), 0.5 * D ** -0.5),
        'final_mod_b': nrm((2 * D,), 0.02),
    }


def reference(x, c, mod_w, mod_b, norm_mix_g, norm_ffn_g, ret_w_in, ret_w_out, rwkv_w_in, rwkv_mu, rwkv_w0, rwkv_w2, rwkv_a0, rwkv_a2, rwkv_g2, rwkv_k_k, rwkv_k_a, rwkv_r_k, rwkv_ln_w, rwkv_ln_b, rwkv_w_out, mlstm_w_in, mlstm_conv_w, mlstm_conv_b, mlstm_gate_b, mlstm_norm_g, mlstm_w_out, ffn_w_up, ffn_conv_w, ffn_conv_b, ffn_w_down, final_g, final_mod_w, final_mod_b):
    c_act = jax.nn.silu(c)
    for i in range(DEPTH):
        mod = c_act @ mod_w[i] + mod_b[i]
        sh_m, sc_m, g_m, sh_f, sc_f, g_f = jnp.split(mod, 6, axis=-1)
        h = modulate(rms_norm(x, norm_mix_g[i]), sh_m, sc_m)
        kind, j = i % N_MIXERS, i // N_MIXERS
        if kind == 0:
            y = retention_mixer(h, ret_w_in[j], ret_w_out[j])
        elif kind == 1:
            y = rwkv7_mixer(h, rwkv_w_in[j], rwkv_mu[j], rwkv_w0[j], rwkv_w2[j], rwkv_a0[j], rwkv_a2[j], rwkv_g2[j], rwkv_k_k[j], rwkv_k_a[j], rwkv_r_k[j], rwkv_ln_w[j], rwkv_ln_b[j], rwkv_w_out[j])
        else:
            y = mlstm_mixer(h, mlstm_w_in[j], mlstm_conv_w[j], mlstm_conv_b[j], mlstm_gate_b[j], mlstm_norm_g[j], mlstm_w_out[j])
        x = x + g_m[:, None, :] * y
        h = modulate(rms_norm(x, norm_ffn_g[i]), sh_f, sc_f)
        x = x + g_f[:, None, :] * conv_glu_ffn(h, ffn_w_up[i], ffn_conv_w[i], ffn_conv_b[i], ffn_w_down[i])
    sh, sc = jnp.split(c_act @ final_mod_w + final_mod_b, 2, axis=-1)
    return modulate(rms_norm(x, final_g), sh, sc)
```

```python
from contextlib import ExitStack
import numpy as np
import concourse.bass as bass
import concourse.mybir as mybir
from concourse.bass_utils import run_bass_kernel_spmd

F32 = mybir.dt.float32
BF16 = mybir.dt.bfloat16
AF = mybir.ActivationFunctionType
ALU = mybir.AluOpType
AX = mybir.AxisListType

D = 2048
B = 4
T = 4096
NT = 2048
DC = D // 128
DFF = 5632
FC = DFF // 128
EPS = 1e-6
TT = 512
NCORES = 8
NMODCH = (4 * 6 * D + 2 * D) // 128
WKSTEP = 16
WDSTEP = 44
DBG_NOLOAD = False


class Buf:
    __slots__ = ("w", "r", "name")

    def __init__(self, name=""):
        self.w = None
        self.r = {}
        self.name = name


class Sched:
    def __init__(self, nc, es, ndma=32):
        self.nc = nc
        self.es = es
        self.eng = {"pe": nc.tensor, "act": nc.scalar, "dve": nc.vector, "pool": nc.gpsimd, "sp": nc.sync}
        self.sems = {}
        for k in self.eng:
            self.sems[k] = es.enter_context(nc.semaphore("sem_" + k))
        self.cnt = {k: 0 for k in self.eng}
        self.seen = {k: {} for k in self.eng}
        self.dval = [0] * (2 * ndma)
        self.dnext = {"hw": 0, "sw": 0}
        for i in range(2 * ndma):
            self.sems[("d", i)] = es.enter_context(nc.semaphore(f"dsem{i}"))
        self.ndma = ndma
        self.uid = 0
        self.out_tokens = []

    def sbuf(self, es, shape, dtype, name="t"):
        self.uid += 1
        t = es.enter_context(self.nc.sbuf_tensor(f"{name}_{self.uid}", list(shape), dtype))
        return t, Buf(name)

    def psum(self, es, shape, dtype, name="ps"):
        self.uid += 1
        t = es.enter_context(self.nc.psum_tensor(f"{name}_{self.uid}", list(shape), dtype))
        return t, Buf(name)

    def _deps(self, reads, writes):
        toks = []
        for b in reads:
            if b.w is not None:
                toks.append(b.w)
        for b in writes:
            if b.w is not None:
                toks.append(b.w)
            toks.extend(b.r.items())
        return toks

    def _wait(self, eng, toks):
        need = {}
        for k, v in toks:
            if k == "pe" and eng == "pe":
                continue
            if need.get(k, 0) < v:
                need[k] = v
        seen = self.seen[eng]
        for k, v in need.items():
            if seen.get(k, 0) < v:
                self.eng[eng].wait_ge(self.sems[k], v)
                seen[k] = v

    def _mark(self, tok, reads, writes):
        k, v = tok
        for b in reads:
            if b.r.get(k, 0) < v:
                b.r[k] = v
        for b in writes:
            b.w = tok
            b.r = {}

    def op(self, eng, fn, reads=(), writes=(), sig=True):
        self._wait(eng, self._deps(reads, writes))
        ins = fn(self.eng[eng])
        if sig:
            self.cnt[eng] += 1
            ins.then_inc(self.sems[eng], 1)
            tok = (eng, self.cnt[eng])
        else:
            tok = (eng, self.cnt[eng] + 1)
        self._mark(tok, reads, writes)
        return ins

    def dma(self, q, out, in_, reads=(), writes=(), is_output=False, **kw):
        cls = "sw" if q == "pool" else "hw"
        j = self.dnext[cls]
        self.dnext[cls] = (j + 1) % self.ndma
        i = j + (self.ndma if cls == "sw" else 0)
        key = ("d", i)
        toks = self._deps(reads, writes)
        if self.dval[i] > 0:
            toks.append((key, self.dval[i]))
        self._wait(q, toks)
        self.dval[i] += 16
        self.eng[q].dma_start(out=out, in_=in_, **kw).then_inc(self.sems[key], 16)
        tok = (key, self.dval[i])
        self._mark(tok, reads, writes)
        if is_output:
            self.out_tokens.append(tok)

    def barrier(self):
        toks = [(("d", i), self.dval[i]) for i in range(2 * self.ndma) if self.dval[i] > 0]
        toks += [(k, self.cnt[k]) for k in self.eng if self.cnt[k] > 0]
        for e in self.eng:
            self._wait(e, [t for t in toks if t[0] != e])

    def finish(self):
        toks = [(("d", i), self.dval[i]) for i in range(2 * self.ndma) if self.dval[i] > 0]
        toks += [(k, self.cnt[k]) for k in self.eng if self.cnt[k] > 0 and k != "sp"]
        self._wait("sp", toks)

    def mm(self, psb, out, lhsT, rhs, start, stop, reads, sig=None):
        return self.op("pe", lambda e: e.matmul(out, lhsT, rhs, start=start, stop=stop),
                       reads=reads, writes=[psb], sig=(stop if sig is None else sig))

    def transpose(self, psb, out, in_, ident, reads):
        return self.op("pe", lambda e: e.transpose(out, in_, ident), reads=reads, writes=[psb])

    def act(self, out, in_, func, reads, writes, **kw):
        return self.op("act", lambda e: e.activation(out=out, in_=in_, func=func, **kw), reads=reads, writes=writes)

    def tt(self, out, in0, in1, op, reads, writes, eng="dve"):
        return self.op(eng, lambda e: e.tensor_tensor(out, in0, in1, op), reads=reads, writes=writes)

    def ts(self, out, in0, s1, s2, op0, op1, reads, writes, eng="dve"):
        if s2 is None:
            return self.op(eng, lambda e: e.tensor_scalar(out, in0, s1, None, op0), reads=reads, writes=writes)
        return self.op(eng, lambda e: e.tensor_scalar(out, in0, s1, s2, op0, op1), reads=reads, writes=writes)

    def stt(self, out, in0, scalar, in1, op0, op1, reads, writes):
        return self.op("dve", lambda e: e.scalar_tensor_tensor(out, in0, scalar, in1, op0, op1),
                       reads=reads, writes=writes)

    def copy(self, eng, out, in_, reads, writes):
        if eng == "act":
            return self.op("act", lambda e: e.copy(out, in_), reads=reads, writes=writes)
        return self.op(eng, lambda e: e.tensor_copy(out, in_), reads=reads, writes=writes)


class Ctx:
    def __init__(self, nc, es):
        self.nc = nc
        self.es = es
        self.s = Sched(nc, es)
        s = self.s
        self.ps = [s.psum(es, [128, 512], F32, f"psb{i}") for i in range(8)]
        self.ones_bf, self.ones_b = s.sbuf(es, [128, 128], BF16, "ones")
        s.op("dve", lambda e: e.memset(self.ones_bf[:], 1.0), writes=[self.ones_b])
        self.dram_bufs = {}

    def dbuf(self, key):
        if key not in self.dram_bufs:
            self.dram_bufs[key] = Buf(str(key))
        return self.dram_bufs[key]


def dram(nc, name, shape, dtype, kind):
    return nc.dram_tensor(name, list(shape), dtype, kind=kind).ap()


def load_weight_bf16(s, wt, wb, src, kc_total, reads=(), kstep=WKSTEP):
    if DBG_NOLOAD:
        return
    for k0 in range(0, kc_total, kstep):
        k1 = min(kc_total, k0 + kstep)
        s.dma("pool", wt[:, k0:k1, :], src[:, k0:k1, :], reads=reads, writes=[wb])


def dma_c(s, q, out, in_, nch, step=8, **kw):
    for k0 in range(0, nch, step):
        k1 = min(nch, k0 + step)
        s.dma(q, out[:, k0:k1, :], in_[:, k0:k1, :], **kw)


MCH = NMODCH // NCORES


def emit_mod(cx, c_in, w_in, b_in, out, modv_dst=None):
    s = cx.s
    with ExitStack() as es:
        ct, cb = s.sbuf(es, [128, DC, 4], F32, "c_t")
        cat, cab = s.sbuf(es, [128, DC, 4], F32, "cact")
        bt, bb = s.sbuf(es, [128, MCH], F32, "modb")
        ot, ob = s.sbuf(es, [128, MCH, 4], F32, "modo")
        for b in range(4):
            s.dma("sp", ct[:, :, b], c_in[b, :].rearrange("(c p) -> p c", p=128), writes=[cb],
                  allow_slow_non_contiguous=True)
        s.dma("sp", bt[:], b_in.rearrange("(c p) -> p c", p=128), writes=[bb], allow_slow_non_contiguous=True)
        s.act(cat[:], ct[:], AF.Silu, reads=[cb], writes=[cab])
        wsrc = w_in.rearrange("(k p) n -> p k n", p=128)
        wts = [s.sbuf(es, [128, DC, 512], F32, f"modw{i}") for i in range(2)]
        pst, psb = cx.ps[0]
        ngroups = (MCH * 128) // 512
        for g in range(ngroups):
            wt, wb = wts[g % 2]
            if not DBG_NOLOAD:
                s.dma("sp" if g % 2 == 0 else "act", wt[:], wsrc[:, :, g * 512:(g + 1) * 512], writes=[wb])
            for j in range(4):
                cc = g * 4 + j
                for k in range(DC):
                    s.mm(psb, pst[:, cc * 4:(cc + 1) * 4], wt[:, k, j * 128:(j + 1) * 128], cat[:, k, :],
                         start=(k == 0), stop=(k == DC - 1), reads=[wb, cab])
        s.tt(ot[:], pst[:, 0:MCH * 4].rearrange("p (c b) -> p c b", b=4),
             bt[:].unsqueeze(2).to_broadcast([128, MCH, 4]), ALU.add, reads=[psb, bb], writes=[ob])
        if modv_dst is not None:
            (mt_, mb_), off = modv_dst
            s.copy("dve", mt_[:, off:off + MCH], ot[:, :, 0], reads=[ob], writes=[mb_])
        else:
            s.dma("sp", out.rearrange("c p b -> p c b"), ot[:], reads=[ob], writes=[cx.dbuf("modout")],
                  is_output=True, allow_slow_non_contiguous=True)


def load_vecs(cx, es, modv_in, gv_in):
    s = cx.s
    mt, mb = s.sbuf(es, [128, NMODCH], F32, "modv")
    gt, gb = s.sbuf(es, [128, 9 * DC], F32, "gv")
    s.dma("sp", mt[:], modv_in, writes=[mb])
    s.dma("sp", gt[:], gv_in, writes=[gb])
    return (mt, mb), (gt, gb)


def make_gmod(cx, es, modv, gv, sc_off, g_off, name):
    s = cx.s
    (mt, mb), (gt, gb) = modv, gv
    t, b = s.sbuf(es, [128, DC], F32, name)
    s.stt(t[:], mt[:, sc_off:sc_off + DC], 1.0, gt[:, g_off:g_off + DC], ALU.add, ALU.mult,
          reads=[mb, gb], writes=[b])
    return t, b


class NormState:
    def __init__(self, cx, es):
        s = cx.s
        self.sq = [s.sbuf(es, [128, TT], BF16, f"sq{i}") for i in range(2)]
        self.rs, self.rsb = s.sbuf(es, [128, TT], F32, "rstd")
        self.tmp = [s.sbuf(es, [128, TT], F32, f"ntmp{i}") for i in range(2)]
        self.epsb = s.sbuf(es, [128, 1], F32, "epsb")
        s.op("dve", lambda e: e.memset(self.epsb[0][:], EPS), writes=[self.epsb[1]])
        self.n = 0


def norm_stat_mm(cx, ns, xt, xb, m, stat):
    s = cx.s
    sq, sqb = ns.sq[m % 2]
    s.act(sq[:], xt[:, m, :], AF.Square, reads=[xb], writes=[sqb])
    pst, psb = stat
    s.mm(psb, pst[:, :], cx.ones_bf[:], sq[:], start=(m == 0), stop=(m == DC - 1), reads=[sqb, cx.ones_b], sig=True)


def norm_finish(cx, ns, xt, xb, stat, gmod, shift_ap_fn, shift_b, out_fn, post_fn=None):
    s = cx.s
    pst, psb = stat
    gm_t, gm_b = gmod
    s.act(ns.rs[:], pst[:, :], AF.Sqrt, reads=[psb, ns.epsb[1]], writes=[ns.rsb], scale=1.0 / D, bias=ns.epsb[0][:])
    s.op("dve", lambda e: e.reciprocal(ns.rs[:], ns.rs[:]), reads=[ns.rsb], writes=[ns.rsb])
    for m in range(DC):
        tmp, tb = ns.tmp[m % 2]
        s.tt(tmp[:], xt[:, m, :], ns.rs[:], ALU.mult, reads=[xb, ns.rsb], writes=[tb])
        o_ap, o_b = out_fn(m)
        s.act(o_ap, tmp[:], AF.Identity, reads=[tb, gm_b, shift_b], writes=[o_b],
              scale=gm_t[:, m:m + 1], bias=shift_ap_fn(m))
        if post_fn is not None:
            post_fn(m)


def emit_prenorm(cx, x_fm, h_out, modv, gv, layer, hkey):
    s = cx.s
    base = 96 * layer
    with ExitStack() as es:
        ns = NormState(cx, es)
        gmod = make_gmod(cx, es, modv, gv, base + 16, layer * DC, "gmodA")
        xts = [s.sbuf(es, [128, DC, TT], F32, f"xA{i}") for i in range(2)]
        hts = [s.sbuf(es, [128, DC, TT], BF16, f"hA{i}") for i in range(2)]
        stat = cx.ps[7]
        for tt in range(NT // TT):
            t0 = tt * TT
            xt, xb = xts[tt % 2]
            ht, hb = hts[tt % 2]
            dma_c(s, "sp", xt[:], x_fm[:, :, t0:t0 + TT].rearrange("c p t -> p c t"), DC, reads=[cx.dbuf("x")], writes=[xb])
            for m in range(DC):
                norm_stat_mm(cx, ns, xt, xb, m, stat)
            norm_finish(cx, ns, xt, xb, stat, gmod, lambda m: modv[0][:, base + m:base + m + 1], modv[1],
                        lambda m: (ht[:, m, :], hb))
            dma_c(s, "act", h_out[:, :, t0:t0 + TT].rearrange("c p t -> p c t"), ht[:], DC, reads=[hb],
                  writes=[cx.dbuf(hkey)], is_output=True)


def run_pipeline(items, dist=1):
    if not items:
        return
    for i in range(min(dist, len(items))):
        items[i][0]()
    for i, (_, comp) in enumerate(items):
        if i + dist < len(items):
            items[i + dist][0]()
        comp()


def emit_c1(cx, x_fm, o_x, w_out, xmid_out, hf_out, modv, gv, layer, KC):
    s = cx.s
    base = 96 * layer
    mt, mb = modv
    with ExitStack() as es:
        ns = NormState(cx, es)
        gmod = make_gmod(cx, es, modv, gv, base + 64, (4 + layer) * DC, "gmodC1")
        ots = [s.sbuf(es, [128, KC, TT], BF16, f"oC{i}") for i in range(2)]
        wts = [s.sbuf(es, [128, KC, 256], BF16, f"woC{i}") for i in range(3)]
        xcs = [s.sbuf(es, [128, TT], F32, f"xcC{i}") for i in range(3)]
        xm, xmb = s.sbuf(es, [128, DC, TT], F32, "xmC")
        ht, hb = s.sbuf(es, [128, DC, TT], BF16, "hC")
        wsrc = w_out.rearrange("(k p) n -> p k n", p=128)
        stat = cx.ps[7]
        NTT = NT // TT
        items = []
        state = {"pending": None}

        def load_o(tt):
            ot, ob = ots[tt % 2]
            t0 = tt * TT
            for k0 in range(0, KC, 8):
                s.dma("sp", ot[:, k0:k0 + 8, :], o_x[k0:k0 + 8, :, t0:t0 + TT].rearrange("k p t -> p k t"),
                      reads=[cx.dbuf("o_x")], writes=[ob])

        def mk(tt, mg, wi):
            t0 = tt * TT
            ot, ob = ots[tt % 2]
            wt, wb = wts[wi % 3]

            def load():
                if tt == 0 and mg == 0:
                    load_o(0)
                if mg == 2 and tt + 1 < NTT:
                    load_o(tt + 1)
                load_weight_bf16(s, wt, wb, wsrc[:, :, mg * 256:(mg + 1) * 256], KC)

            def comp():
                for j in range(2):
                    m = mg * 2 + j
                    pst, psb = cx.ps[m % 4]
                    for k in range(KC):
                        s.mm(psb, pst[:, :], wt[:, k, j * 128:(j + 1) * 128], ot[:, k, :],
                             start=(k == 0), stop=(k == KC - 1), reads=[wb, ob])
                    if state["pending"] is not None:
                        norm_stat_mm(cx, ns, xm, xmb, state["pending"], stat)
                    xc, xcb = xcs[m % 3]
                    s.dma("sp", xc[:], x_fm[m, :, t0:t0 + TT], reads=[cx.dbuf("x")], writes=[xcb])
                    s.stt(xm[:, m, :], pst[:, :], mt[:, base + 32 + m:base + 33 + m], xc[:], ALU.mult, ALU.add,
                          reads=[psb, mb, xcb], writes=[xmb])
                    s.dma("act", xmid_out[m, :, t0:t0 + TT], xm[:, m, :], reads=[xmb], writes=[cx.dbuf("xmid")],
                          is_output=True)
                    state["pending"] = m
                if mg == DC // 2 - 1:
                    norm_stat_mm(cx, ns, xm, xmb, state["pending"], stat)
                    state["pending"] = None
                    norm_finish(cx, ns, xm, xmb, stat, gmod, lambda m: mt[:, base + 48 + m:base + 49 + m], mb,
                                lambda m: (ht[:, m, :], hb))
                    dma_c(s, "act", hf_out[:, :, t0:t0 + TT].rearrange("c p t -> p c t"), ht[:], DC, reads=[hb],
                          writes=[cx.dbuf("hf")], is_output=True)
            return load, comp

        wi = 0
        for tt in range(NTT):
            for mg in range(DC // 2):
                items.append(mk(tt, mg, wi))
                wi += 1
        run_pipeline(items, dist=2)


def emit_c2(cx, xmid_fm, hf_fm, halo_in, w_up, conv_w, conv_b, w_down, xnew_out, hnext_out,
            modv, gv, layer, final):
    s = cx.s
    base = 96 * layer
    mt, mb = modv
    with ExitStack() as es:
        ns = NormState(cx, es)
        if final:
            gmod = make_gmod(cx, es, modv, gv, 400, 8 * DC, "gmodF")
            sh_off = 384
        else:
            gmod = make_gmod(cx, es, modv, gv, base + 96 + 16, (layer + 1) * DC, "gmodN")
            sh_off = base + 96
        cwt, cwb = s.sbuf(es, [128, 3 * FC], F32, "convw")
        cbt, cbb = s.sbuf(es, [128, FC], F32, "convb")
        s.dma("sp", cwt[:], conv_w, writes=[cwb])
        s.dma("sp", cbt[:], conv_b, writes=[cbb])
        hfts = [s.sbuf(es, [128, DC, TT], BF16, f"hf{i}") for i in range(1)]
        halo, halob = s.sbuf(es, [128, DC, 2], BF16, "halo")
        carry, carryb = s.sbuf(es, [128, FC, 2], F32, "carry")
        hid, hidb = s.sbuf(es, [128, FC, TT], BF16, "hid")
        NWB = 3
        wgs = [s.sbuf(es, [128, DC, 256], BF16, f"wg{i}") for i in range(NWB)]
        wvs = [s.sbuf(es, [128, DC, 256], BF16, f"wv{i}") for i in range(NWB)]
        wds = [s.sbuf(es, [128, FC, 128], BF16, f"wd{i}") for i in range(NWB)]
        gps = [s.sbuf(es, [128, TT + 2], F32, f"gp{i}") for i in range(2)]
        t1s = [s.sbuf(es, [128, TT], F32, f"t1{i}") for i in range(2)]
        sgs = [s.sbuf(es, [128, TT], F32, f"sg{i}") for i in range(2)]
        xcs = [s.sbuf(es, [128, TT], F32, f"xc2{i}") for i in range(2)]
        xn, xnb = s.sbuf(es, [128, DC, TT], F32, "xn")
        hos = [s.sbuf(es, [128, TT], F32 if final else BF16, f"hoF{i}") for i in range(2)]
        s.dma("sp", halo[:], halo_in.rearrange("c p t -> p c t"), writes=[halob], allow_slow_non_contiguous=True)
        wu_src = w_up.rearrange("(k p) n -> p k n", p=128)
        wd_src = w_down.rearrange("(k p) n -> p k n", p=128)
        stat = cx.ps[7]
        NTT = NT // TT
        items = []
        state = {"pending": None, "wi": 0, "di": 0}

        def load_hf(tt):
            hft, hfb = hfts[0]
            t0 = tt * TT
            dma_c(s, "sp", hft[:], hf_fm[:, :, t0:t0 + TT].rearrange("c p t -> p c t"), DC, reads=[cx.dbuf("hf")],
                  writes=[hfb])

        def mk_halo(jg):
            wi = state["wi"]
            state["wi"] += 1
            wg, wgb = wgs[wi % NWB]

            def load():
                load_weight_bf16(s, wg, wgb, wu_src[:, :, jg * 256:(jg + 1) * 256], DC)

            def comp():
                for jj in range(2):
                    j = jg * 2 + jj
                    pst, psb = cx.ps[j % 2]
                    for k in range(DC):
                        s.mm(psb, pst[:, 0:2], wg[:, k, jj * 128:(jj + 1) * 128], halo[:, k, :],
                             start=(k == 0), stop=(k == DC - 1), reads=[wgb, halob])
                    s.copy("dve", carry[:, j, :], pst[:, 0:2], reads=[psb], writes=[carryb])
            return load, comp

        def mk_up(tt, jg):
            wi = state["wi"]
            state["wi"] += 1
            wg, wgb = wgs[wi % NWB]
            wv, wvb = wvs[wi % NWB]
            hft, hfb = hfts[0]

            def load():
                if tt == 0 and jg == 0:
                    load_hf(0)
                load_weight_bf16(s, wg, wgb, wu_src[:, :, jg * 256:(jg + 1) * 256], DC)
                load_weight_bf16(s, wv, wvb, wu_src[:, :, DFF + jg * 256:DFF + (jg + 1) * 256], DC)

            def comp():
                for jj in range(2):
                    j = jg * 2 + jj
                    pg, pgb = cx.ps[(2 * j) % 6]
                    pv, pvb = cx.ps[(2 * j + 1) % 6]
                    for k in range(DC):
                        s.mm(pgb, pg[:, :], wg[:, k, jj * 128:(jj + 1) * 128], hft[:, k, :],
                             start=(k == 0), stop=(k == DC - 1), reads=[wgb, hfb])
                    for k in range(DC):
                        s.mm(pvb, pv[:, :], wv[:, k, jj * 128:(jj + 1) * 128], hft[:, k, :],
                             start=(k == 0), stop=(k == DC - 1), reads=[wvb, hfb])
                    gp, gpb = gps[j % 2]
                    t1, t1b = t1s[j % 2]
                    sg, sgb = sgs[j % 2]
                    s.copy("act", gp[:, 2:TT + 2], pg[:, :], reads=[pgb], writes=[gpb])
                    s.copy("act", gp[:, 0:2], carry[:, j, :], reads=[carryb], writes=[gpb])
                    s.copy("act", carry[:, j, :], gp[:, TT:TT + 2], reads=[gpb], writes=[carryb])
                    s.ts(t1[:], gp[:, 0:TT], cwt[:, j:j + 1], cbt[:, j:j + 1], ALU.mult, ALU.add,
                         reads=[gpb, cwb, cbb], writes=[t1b])
                    s.stt(t1[:], gp[:, 1:TT + 1], cwt[:, FC + j:FC + j + 1], t1[:], ALU.mult, ALU.add,
                          reads=[gpb, cwb, t1b], writes=[t1b])
                    s.stt(t1[:], gp[:, 2:TT + 2], cwt[:, 2 * FC + j:2 * FC + j + 1], t1[:], ALU.mult, ALU.add,
                          reads=[gpb, cwb, t1b], writes=[t1b])
                    s.act(sg[:], t1[:], AF.Silu, reads=[t1b], writes=[sgb])
                    s.tt(hid[:, j, :], sg[:], pv[:, :], ALU.mult, reads=[sgb, pvb], writes=[hidb])
            return load, comp

        def mk_down(tt, m):
            di = state["di"]
            state["di"] += 1
            wd, wdb = wds[di % NWB]
            t0 = tt * TT

            def load():
                if m == 2 and tt + 1 < NTT:
                    load_hf(tt + 1)
                load_weight_bf16(s, wd, wdb, wd_src[:, :, m * 128:(m + 1) * 128], FC, kstep=WDSTEP)

            def comp():
                pst, psb = cx.ps[m % 4]
                for k in range(FC):
                    s.mm(psb, pst[:, :], wd[:, k, :], hid[:, k, :], start=(k == 0), stop=(k == FC - 1),
                         reads=[wdb, hidb])
                if state["pending"] is not None:
                    norm_stat_mm(cx, ns, xn, xnb, state["pending"], stat)
                xc, xcb = xcs[m % 2]
                s.dma("sp", xc[:], xmid_fm[m, :, t0:t0 + TT], reads=[cx.dbuf("xmid")], writes=[xcb])
                s.stt(xn[:, m, :], pst[:, :], mt[:, base + 80 + m:base + 81 + m], xc[:], ALU.mult, ALU.add,
                      reads=[psb, mb, xcb], writes=[xnb])
                if not final:
                    s.dma("act", xnew_out[m, :, t0:t0 + TT], xn[:, m, :], reads=[xnb], writes=[cx.dbuf("xnew")],
                          is_output=True)
                state["pending"] = m
                if m == DC - 1:
                    norm_stat_mm(cx, ns, xn, xnb, state["pending"], stat)
                    state["pending"] = None
                    def post(mm_):
                        s.dma("act", hnext_out[mm_, :, t0:t0 + TT], hos[mm_ % 2][0][:], reads=[hos[mm_ % 2][1]],
                              writes=[cx.dbuf("hnext")], is_output=True)
                    norm_finish(cx, ns, xn, xnb, stat, gmod, lambda mm_: mt[:, sh_off + mm_:sh_off + mm_ + 1], mb,
                                lambda mm_: (hos[mm_ % 2][0][:], hos[mm_ % 2][1]), post)
            return load, comp

        for jg in range(FC // 2):
            items.append(mk_halo(jg))
        for tt in range(NTT):
            for jg in range(FC // 2):
                items.append(mk_up(tt, jg))
            for m in range(DC):
                items.append(mk_down(tt, m))
        run_pipeline(items, dist=NWB - 1)


def to_fm(a):
    ntok, f = a.shape
    return np.ascontiguousarray(a.T.reshape(f // 128, 128, ntok))


def from_fm(a):
    c, p, ntok = a.shape
    return np.ascontiguousarray(a.reshape(c * p, ntok).T)


def vec_cols(v):
    v = np.asarray(v, np.float32).reshape(-1, 128)
    return np.ascontiguousarray(v.T)


def new_prog():
    nc = bass.Bass("TRN2", target_bir_lowering=False)
    return nc


def run(nc, in_maps):
    res = run_bass_kernel_spmd(nc, in_maps, core_ids=list(range(len(in_maps))))
    return res.results


def build_mod_prog():
    nc = new_prog()
    c_in = dram(nc, "c", [4, D], F32, "ExternalInput")
    w_in = dram(nc, "w", [D, MCH * 128], F32, "ExternalInput")
    b_in = dram(nc, "b", [MCH * 128], F32, "ExternalInput")
    out = dram(nc, "out", [MCH, 128, 4], F32, "ExternalOutput")
    with ExitStack() as es:
        cx = Ctx(nc, es)
        emit_mod(cx, c_in, w_in, b_in, out)
        cx.s.finish()
    return nc


def host_mod_inputs(c, mod_w, mod_b, final_mod_w, final_mod_b):
    wcat = np.concatenate([mod_w[i] for i in range(4)] + [final_mod_w], axis=1)
    bcat = np.concatenate([mod_b[i] for i in range(4)] + [final_mod_b], axis=0)
    n = MCH * 128
    return [{"c": c, "w": np.ascontiguousarray(wcat[:, j * n:(j + 1) * n]), "b": np.ascontiguousarray(bcat[j * n:(j + 1) * n])}
            for j in range(NCORES)]


def host_modv(res):
    allm = np.concatenate([r["out"] for r in res], axis=0)
    return [np.ascontiguousarray(allm[:, :, b].T) for b in range(4)]


def build_a0_prog(layer=0):
    nc = new_prog()
    x_fm = dram(nc, "x_fm", [DC, 128, NT], F32, "ExternalInput")
    modv_in = dram(nc, "modv", [128, NMODCH], F32, "ExternalInput")
    gv_in = dram(nc, "gv", [128, 9 * DC], F32, "ExternalInput")
    h_out = dram(nc, "h_out", [DC, 128, NT], BF16, "ExternalOutput")
    with ExitStack() as es:
        cx = Ctx(nc, es)
        modv, gv = load_vecs(cx, es, modv_in, gv_in)
        emit_prenorm(cx, x_fm, h_out, modv, gv, layer, "h_out")
        cx.s.finish()
    return nc


def build_c1_prog(layer, KC):
    nc = new_prog()
    x_fm = dram(nc, "x_fm", [DC, 128, NT], F32, "ExternalInput")
    o_x = dram(nc, "o_x", [KC, 128, NT], BF16, "ExternalInput")
    w_out = dram(nc, "w_out", [KC * 128, D], F32, "ExternalInput")
    modv_in = dram(nc, "modv", [128, NMODCH], F32, "ExternalInput")
    gv_in = dram(nc, "gv", [128, 9 * DC], F32, "ExternalInput")
    xmid = dram(nc, "xmid", [DC, 128, NT], F32, "ExternalOutput")
    hf = dram(nc, "hf", [DC, 128, NT], BF16, "ExternalOutput")
    with ExitStack() as es:
        cx = Ctx(nc, es)
        modv, gv = load_vecs(cx, es, modv_in, gv_in)
        emit_c1(cx, x_fm, o_x, w_out, xmid, hf, modv, gv, layer, KC)
        cx.s.finish()
    return nc


def build_c2_prog(layer, final):
    nc = new_prog()
    xmid = dram(nc, "xmid", [DC, 128, NT], F32, "ExternalInput")
    hf = dram(nc, "hf", [DC, 128, NT], BF16, "ExternalInput")
    halo = dram(nc, "halo", [DC, 128, 2], BF16, "ExternalInput")
    w_up = dram(nc, "w_up", [D, 2 * DFF], F32, "ExternalInput")
    conv_w = dram(nc, "conv_w", [128, 3 * FC], F32, "ExternalInput")
    conv_b = dram(nc, "conv_b", [128, FC], F32, "ExternalInput")
    w_down = dram(nc, "w_down", [DFF, D], F32, "ExternalInput")
    modv_in = dram(nc, "modv", [128, NMODCH], F32, "ExternalInput")
    gv_in = dram(nc, "gv", [128, 9 * DC], F32, "ExternalInput")
    xnew = dram(nc, "xnew", [DC, 128, NT], F32, "ExternalOutput")
    hnext = dram(nc, "hnext", [DC, 128, NT], F32 if final else BF16, "ExternalOutput")
    with ExitStack() as es:
        cx = Ctx(nc, es)
        modv, gv = load_vecs(cx, es, modv_in, gv_in)
        emit_c2(cx, xmid, hf, halo, w_up, conv_w, conv_b, w_down, xnew, hnext, modv, gv, layer, final)
        cx.s.finish()
    return nc


def host_gv(norm_mix_g, norm_ffn_g, final_g):
    return vec_cols(np.concatenate([norm_mix_g.reshape(-1), norm_ffn_g.reshape(-1), final_g.reshape(-1)]))


def host_conv(conv_w, conv_b):
    return vec_cols(conv_w.reshape(-1)), vec_cols(conv_b)


NBLK = T // TT
CPB = TT // 128


def interleave(a, b):
    n = max(len(a), len(b))
    for i in range(n):
        if i < len(a):
            a[i]()
        if i < len(b):
            b[i]()


def head_norm_stats(s, st, sum_t, sumsq_t, n, eps, epsb):
    t, b = st
    s.ts(t[:, 0:1], t[:, 4:5], 1.0 / n, None, ALU.mult, None, reads=[b], writes=[b])
    s.tt(t[:, 1:2], t[:, 0:1], t[:, 0:1], ALU.mult, reads=[b], writes=[b])
    s.stt(t[:, 2:3], t[:, 5:6], 1.0 / n, t[:, 1:2], ALU.mult, ALU.subtract, reads=[b], writes=[b])
    s.act(t[:, 3:4], t[:, 2:3], AF.Sqrt, reads=[b, epsb[1]], writes=[b], bias=epsb[0][:], scale=1.0)
    s.op("dve", lambda e: e.reciprocal(t[:, 3:4], t[:, 3:4]), reads=[b], writes=[b])


def emit_ret(cx, h_full, w_in, cos_in, sin_in, rdec_in, gl_in, mask_in, ident_in, o_out):
    s = cx.s
    NH = 4
    with ExitStack() as es:
        ident, identb = s.sbuf(es, [128, 128], BF16, "ident")
        mask, maskb = s.sbuf(es, [128, 128], F32, "maskT")
        rdec, rdecb = s.sbuf(es, [128, 2, NH, 128], F32, "rdec")
        gl, glb = s.sbuf(es, [128, NH], F32, "gl")
        epsb = s.sbuf(es, [128, 1], F32, "epsr")
        s.op("dve", lambda e: e.memset(epsb[0][:], EPS), writes=[epsb[1]])
        s.dma("sp", ident[:], ident_in, writes=[identb])
        s.dma("sp", mask[:], mask_in, writes=[maskb])
        s.dma("sp", rdec[:], rdec_in.rearrange("p (a h t) -> p a h t", a=2, h=NH), writes=[rdecb])
        s.dma("sp", gl[:], gl_in, writes=[glb])
        ht, hb = s.sbuf(es, [128, DC, TT], BF16, "hR")
        wring = [s.sbuf(es, [128, DC, 512], BF16, f"wR{i}") for i in range(4)]
        cst = [s.sbuf(es, [128, 2, TT], F32, f"cs{i}") for i in range(2)]
        tabs = [s.sbuf(es, [128, 4, TT], F32, f"tab{i}") for i in range(2)]
        qk_fm = [s.sbuf(es, [128, 2, 2, TT], BF16, f"qkfm{i}") for i in range(2)]
        k_tm = [s.sbuf(es, [128, CPB, 256], BF16, f"ktm{i}") for i in range(2)]
        v_tm = [s.sbuf(es, [128, CPB, 512], BF16, f"vtm{i}") for i in range(2)]
        g_tm = [s.sbuf(es, [128, CPB, 512], BF16, f"gtm{i}") for i in range(2)]
        S = [s.sbuf(es, [128, 2, 512], F32, f"S{h}") for h in range(NH)]
        Sb = [s.sbuf(es, [128, 2, 512], BF16, f"Sb{h}") for h in range(NH)]
        for h in range(NH):
            s.op("dve", lambda e, h=h: e.memset(S[h][0][:], 0.0), writes=[S[h][1]])
            s.op("dve", lambda e, h=h: e.memset(Sb[h][0][:], 0.0), writes=[Sb[h][1]])
        rt = [s.sbuf(es, [128, TT], F32, f"rt{i}") for i in range(4)]
        sc_sb = [s.sbuf(es, [128, 128], BF16, f"scsb{i}") for i in range(2)]
        o_sb = [s.sbuf(es, [128, 512], F32, f"osb{i}") for i in range(2)]
        junk = s.sbuf(es, [128, 512], BF16, "junk")
        nrm = [s.sbuf(es, [128, 512], F32, f"nrm{i}") for i in range(2)]
        gated = [s.sbuf(es, [128, 512], BF16, f"gated{i}") for i in range(2)]
        stats = [s.sbuf(es, [128, 8], F32, f"st{i}") for i in range(2)]
        oblk = [s.sbuf(es, [128, DC, TT], BF16, f"oblk{i}") for i in range(1)]
        proj_ps = cx.ps[0:4]
        b4t, b4b = cx.ps[4]
        b4bf = b4t[:].bitcast(BF16)
        kT_ps = (b4bf[:, 0:256], b4b)
        oT_ps = (b4bf[:, 256:768], b4b)
        sc_ps = (b4t[:, 384:512], b4b)
        o_ps = cx.ps[5]
        S_ps = [cx.ps[6], cx.ps[7]]
        wsrc = w_in.rearrange("(k p) n -> p k n", p=128)
        st8 = {"pp": 0, "wi": 0}

        wtiles = [(tb, hl, kind) for tb in range(NBLK) for hl in range(NH) for kind in range(3)]

        def wload(idx):
            if idx >= len(wtiles):
                return
            tb, hl, kind = wtiles[idx]
            wt, wb = wring[idx % 4]
            c0 = hl * 1536 + kind * 512
            load_weight_bf16(s, wt, wb, wsrc[:, :, c0:c0 + 512], DC)

        def next_ps():
            p = proj_ps[st8["pp"] % 4]
            st8["pp"] += 1
            return p

        def mk_proj(tb, hl):
            widx = (tb * NH + hl) * 3
            par = (tb * NH + hl) % 2
            tab, tabb = tabs[par]
            cs, csb = cst[tb % 2]
            qkf, qkfb = qk_fm[par]
            vt, vtb = v_tm[par]
            gt, gtb = g_tm[par]

            def tables():
                for a in range(2):
                    for f in range(2):
                        s.tt(tab[:, a * 2 + f, :].rearrange("p (c t) -> p c t", c=CPB),
                             cs[:, f, :].rearrange("p (c t) -> p c t", c=CPB),
                             rdec[:, a, hl:hl + 1, :].to_broadcast([128, CPB, 128]), ALU.mult,
                             reads=[csb, rdecb], writes=[tabb])

            def pair(a):
                wt, wb = wring[widx % 4]
                (p1, p1b), (p2, p2b) = next_ps(), next_ps()
                for half, (pp, ppb) in enumerate(((p1, p1b), (p2, p2b))):
                    c0 = a * 256 + half * 128
                    for k in range(DC):
                        s.mm(ppb, pp[:, :], wt[:, k, c0:c0 + 128], ht[:, k, :], start=(k == 0), stop=(k == DC - 1),
                             reads=[wb, hb])
                c_ap, s_ap = tab[:, a * 2, :], tab[:, a * 2 + 1, :]
                (ta, tab_), (tb_, tbb), (tc, tcb), (td, tdb) = rt
                s.tt(ta[:], p1[:, :], c_ap, ALU.mult, reads=[p1b, tabb], writes=[tab_])
                s.tt(tb_[:], p2[:, :], s_ap, ALU.mult, reads=[p2b, tabb], writes=[tbb])
                s.tt(tc[:], p1[:, :], s_ap, ALU.mult, reads=[p1b, tabb], writes=[tcb])
                s.tt(td[:], p2[:, :], c_ap, ALU.mult, reads=[p2b, tabb], writes=[tdb])
                s.tt(qkf[:, a, 0, :], ta[:], tb_[:], ALU.subtract, reads=[tab_, tbb], writes=[qkfb])
                s.tt(qkf[:, a, 1, :], tc[:], td[:], ALU.add, reads=[tcb, tdb], writes=[qkfb])

            def p_q():
                wload(widx + 2)
                tables()
                pair(0)

            def p_k():
                pair(1)

            def tm_proj(kind):
                wt, wb = wring[(widx + kind) % 4]
                for c4 in range(CPB):
                    pp, ppb = next_ps()
                    for k in range(DC):
                        s.mm(ppb, pp[:, :], ht[:, k, c4 * 128:(c4 + 1) * 128], wt[:, k, :], start=(k == 0),
                             stop=(k == DC - 1), reads=[wb, hb])
                    if kind == 1:
                        s.copy("act", vt[:, c4, :], pp[:, :], reads=[ppb], writes=[vtb])
                    else:
                        s.act(gt[:, c4, :], pp[:, :], AF.Silu, reads=[ppb], writes=[gtb])

            def p_v():
                wload(widx + 3)
                tm_proj(1)

            def p_g():
                wload(widx + 4)
                tm_proj(2)

            return [p_q, p_k, p_v, p_g]

        def mk_recur(tb, hl):
            par = (tb * NH + hl) % 2
            qkf, qkfb = qk_fm[par]
            kt, ktb = k_tm[par]
            vt, vtb = v_tm[par]
            gt, gtb = g_tm[par]
            St, Stb = S[hl]
            Sbt, Sbb = Sb[hl]
            ob, obb = oblk[0]

            def chunk(c4):
                i2 = (tb * NH * CPB + hl * CPB + c4) % 2
                cols = slice(c4 * 128, (c4 + 1) * 128)
                for half in range(2):
                    s.transpose(kT_ps[1], kT_ps[0][:, half * 128:(half + 1) * 128], qkf[:, 1, half, cols], ident[:],
                                reads=[qkfb, identb])
                s.copy("act", kt[:, c4, :], kT_ps[0], reads=[kT_ps[1]], writes=[ktb])
                for half in range(2):
                    s.mm(sc_ps[1], sc_ps[0], qkf[:, 1, half, cols], qkf[:, 0, half, cols], start=(half == 0),
                         stop=(half == 1), reads=[qkfb])
                scs, scsb = sc_sb[i2]
                s.tt(scs[:], sc_ps[0], mask[:], ALU.mult, reads=[sc_ps[1], maskb], writes=[scsb])
                opt, opb = o_ps
                s.mm(opb, opt[:, :], scs[:], vt[:, c4, :], start=True, stop=False, reads=[scsb, vtb], sig=True)
                for half in range(2):
                    s.mm(opb, opt[:, :], qkf[:, 0, half, cols], Sbt[:, half, :], start=False, stop=(half == 1),
                         reads=[qkfb, Sbb], sig=True)
                for half in range(2):
                    spt, spb = S_ps[half]
                    s.mm(spb, spt[:, :], kt[:, c4, half * 128:(half + 1) * 128], vt[:, c4, :], start=True, stop=True,
                         reads=[ktb, vtb])
                for half in range(2):
                    spt, spb = S_ps[half]
                    s.ts(St[:, half, :], St[:, half, :], gl[:, hl:hl + 1], None, ALU.mult, None, reads=[Stb, glb],
                         writes=[Stb])
                    s.stt(St[:, half, :], spt[:, :], gl[:, hl:hl + 1], St[:, half, :], ALU.mult, ALU.add,
                          reads=[spb, glb, Stb], writes=[Stb])
                s.copy("act", Sbt[:], St[:], reads=[Stb], writes=[Sbb])
                osb, osbb = o_sb[i2]
                stt_, stb = stats[i2]
                s.act(osb[:], opt[:, :], AF.Copy, reads=[opb], writes=[osbb, stb], accum_out=stt_[:, 4:5])
                s.act(junk[0][:], opt[:, :], AF.Square, reads=[opb], writes=[junk[1], stb], accum_out=stt_[:, 5:6])
                head_norm_stats(s, (stt_, stb), None, None, 512, EPS, epsb)
                nr, nrb = nrm[i2]
                s.ts(nr[:], osb[:], stt_[:, 0:1], stt_[:, 3:4], ALU.subtract, ALU.mult, reads=[osbb, stb], writes=[nrb])
                ga, gab = gated[i2]
                s.tt(ga[:], nr[:], gt[:, c4, :], ALU.mult, reads=[nrb, gtb], writes=[gab])
                for ec in range(4):
                    s.transpose(oT_ps[1], oT_ps[0][:, ec * 128:(ec + 1) * 128], ga[:, ec * 128:(ec + 1) * 128], ident[:],
                                reads=[gab, identb])
                s.copy("act", ob[:, hl * 4:(hl + 1) * 4, cols],
                       oT_ps[0].rearrange("p (e t) -> p e t", e=4), reads=[oT_ps[1]], writes=[obb])

            return [lambda c4=c4: chunk(c4) for c4 in range(CPB)]

        def block_io_start(tb):
            j, t0 = tb // 4, (tb % 4) * TT
            dma_c(s, "sp", ht[:], h_full[j, :, :, t0:t0 + TT].rearrange("c p t -> p c t"), DC, reads=[cx.dbuf("h_full")],
                  writes=[hb])
            cs, csb = cst[tb % 2]
            g0 = tb * TT
            s.dma("sp", cs[:, 0, :], cos_in[:, g0:g0 + TT], writes=[csb])
            s.dma("sp", cs[:, 1, :], sin_in[:, g0:g0 + TT], writes=[csb])

        def block_io_end(tb):
            j, t0 = tb // 4, (tb % 4) * TT
            ob, obb = oblk[0]
            dma_c(s, "act", o_out[j, :, :, t0:t0 + TT].rearrange("c p t -> p c t"), ob[:], DC, reads=[obb],
                  writes=[cx.dbuf("o_out")], is_output=True)

        wload(0)
        wload(1)
        prev = None
        for tb in range(NBLK):
            for hl in range(NH):
                if hl == 0:
                    pass
                pr = mk_proj(tb, hl)
                if hl == 0:
                    first = pr[0]

                    def p0(first=first, tb=tb):
                        block_io_start(tb)
                        first()
                    pr[0] = p0
                rc = prev if prev is not None else []
                interleave(pr, rc)
                if prev is not None and hl == 0 and tb > 0:
                    block_io_end(tb - 1)
                prev = mk_recur(tb, hl)
        interleave([], prev)
        block_io_end(NBLK - 1)


def build_ret_prog():
    nc = new_prog()
    h_full = dram(nc, "h_full", [2, DC, 128, NT], BF16, "ExternalInput")
    w_in = dram(nc, "w_in", [D, 6144], F32, "ExternalInput")
    cos_in = dram(nc, "cos", [128, T], F32, "ExternalInput")
    sin_in = dram(nc, "sin", [128, T], F32, "ExternalInput")
    rdec = dram(nc, "rdec", [128, 2 * 4 * 128], F32, "ExternalInput")
    gl = dram(nc, "gl", [128, 4], F32, "ExternalInput")
    mask = dram(nc, "mask", [128, 128], F32, "ExternalInput")
    ident = dram(nc, "ident", [128, 128], BF16, "ExternalInput")
    o_out = dram(nc, "o_out", [2, DC, 128, NT], BF16, "ExternalOutput")
    with ExitStack() as es:
        cx = Ctx(nc, es)
        emit_ret(cx, h_full, w_in, cos_in, sin_in, rdec, gl, mask, ident, o_out)
        cx.s.finish()
    return nc


def host_ret_consts(r):
    import ml_dtypes
    i = np.arange(128, dtype=np.float32)
    inv = (np.float32(10000.0) ** (-i / np.float32(128.0))).astype(np.float32)
    pos = np.arange(T, dtype=np.float32)
    ang = (inv[:, None] * pos[None, :]).astype(np.float32)
    cos = np.cos(ang.astype(np.float64)).astype(np.float32)
    sin = np.sin(ang.astype(np.float64)).astype(np.float32)
    tl = np.arange(128, dtype=np.float64)
    rdec = np.zeros((2, 4, 128), np.float64)
    gl = np.zeros((4,), np.float64)
    for hl in range(4):
        gamma = 1.0 - 2.0 ** (-5.0 - (4 * r + hl))
        rdec[0, hl] = gamma ** (tl + 1.0)
        rdec[1, hl] = gamma ** (-(tl + 1.0)) / 16.0
        gl[hl] = gamma ** 128.0
    rdec = np.broadcast_to(rdec.reshape(1, -1), (128, 2 * 4 * 128)).astype(np.float32)
    gl = np.broadcast_to(gl.reshape(1, 4), (128, 4)).astype(np.float32)
    idx = np.arange(128)
    mask = (idx[:, None] <= idx[None, :]).astype(np.float32)
    ident = np.eye(128, dtype=np.float32).astype(ml_dtypes.bfloat16)
    return {"cos": cos, "sin": sin, "rdec": np.ascontiguousarray(rdec), "gl": np.ascontiguousarray(gl),
            "mask": mask, "ident": ident}


def host_ret_w(w_in, r):
    cols = []
    for hl in range(4):
        h = 4 * r + hl
        cols.append(w_in[:, h * 256:(h + 1) * 256])
        cols.append(w_in[:, 2048 + h * 256:2048 + (h + 1) * 256])
        cols.append(w_in[:, 4096 + h * 512:4096 + (h + 1) * 512])
        cols.append(w_in[:, 8192 + h * 512:8192 + (h + 1) * 512])
    return np.ascontiguousarray(np.concatenate(cols, axis=1))


GATE_CAP = 15.0


def emit_mlstm(cx, h_full, w_in, wgate_in, cw_in, cb_in, gb_in, ng_in, mask_in, ident_in, identf_in, o_out):
    s = cx.s
    NH = 2
    with ExitStack() as es:
        ident, identb = s.sbuf(es, [128, 128], BF16, "ident")
        identf, identfb = s.sbuf(es, [128, 128], F32, "identf")
        mask, maskb = s.sbuf(es, [128, 128], F32, "maskT")
        cw, cwb = s.sbuf(es, [128, 32], F32, "mcw")
        cbt, cbb = s.sbuf(es, [128, 8], F32, "mcb")
        gb, gbb = s.sbuf(es, [128, 4], F32, "gb")
        ng, ngb = s.sbuf(es, [128, NH, 512], F32, "ng")
        wg32, wg32b = s.sbuf(es, [128, DC, 4], F32, "wg32")
        wgr, wgrb = s.sbuf(es, [128, DC, 4, 128], BF16, "wgrep")
        epsb = s.sbuf(es, [128, 1], F32, "epsm")
        oneb = s.sbuf(es, [128, 1], F32, "onem")
        ones_r = s.sbuf(es, [128, 128], F32, "ones_r")
        onec = s.sbuf(es, [128, 1], BF16, "onec")
        s.op("dve", lambda e: e.memset(epsb[0][:], EPS), writes=[epsb[1]])
        s.op("dve", lambda e: e.memset(oneb[0][:], 1.0), writes=[oneb[1]])
        s.op("dve", lambda e: e.memset(ones_r[0][:], 1.0), writes=[ones_r[1]])
        s.op("dve", lambda e: e.memset(onec[0][:], 1.0), writes=[onec[1]])
        s.dma("sp", ident[:], ident_in, writes=[identb])
        s.dma("sp", identf[:], identf_in, writes=[identfb])
        s.dma("sp", mask[:], mask_in, writes=[maskb])
        s.dma("sp", cw[:], cw_in, writes=[cwb])
        s.dma("sp", cbt[:], cb_in, writes=[cbb])
        s.dma("sp", gb[:], gb_in, writes=[gbb])
        s.dma("sp", ng[:], ng_in.rearrange("p (h e) -> p h e", h=NH), writes=[ngb])
        s.dma("sp", wg32[:], wgate_in.rearrange("(k p) g -> p k g", p=128), writes=[wg32b],
              allow_slow_non_contiguous=True)
        s.op("dve", lambda e: e.tensor_copy(wgr[:], wg32[:].unsqueeze(3).to_broadcast([128, DC, 4, 128])),
             reads=[wg32b], writes=[wgrb])
        gb15, gb15b = s.sbuf(es, [128, 4], F32, "gb15")
        s.ts(gb15[:], gb[:], 1.0 / GATE_CAP, None, ALU.mult, None, reads=[gbb], writes=[gb15b])

        ht, hb = s.sbuf(es, [128, DC, TT], BF16, "hM")
        wring = [s.sbuf(es, [128, DC, 512], BF16, f"wM{i}") for i in range(4)]
        qk_fm = [s.sbuf(es, [128, 2, 2, TT], BF16, f"mqk{i}") for i in range(2)]
        k_tm = [s.sbuf(es, [128, CPB, 256], BF16, f"mktm{i}") for i in range(2)]
        v_tm = [s.sbuf(es, [128, CPB, 512], BF16, f"mvtm{i}") for i in range(2)]
        o_tm = [s.sbuf(es, [128, CPB, 512], BF16, f"motm{i}") for i in range(2)]
        li15 = [s.sbuf(es, [128, TT], F32, f"li{i}") for i in range(2)]
        nbr = [s.sbuf(es, [128, TT], F32, f"nb{i}") for i in range(2)]
        crow = [s.sbuf(es, [128, TT], F32, f"crow{i}") for i in range(2)]
        gtmp = [s.sbuf(es, [128, TT], F32, f"gtmp{i}") for i in range(2)]
        xp = [s.sbuf(es, [128, TT + 3], F32, f"mxp{i}") for i in range(2)]
        acc = [s.sbuf(es, [128, TT], F32, f"macc{i}") for i in range(2)]
        ccarry, ccarryb = s.sbuf(es, [128, 8, 3], F32, "mcarry")
        s.op("dve", lambda e: e.memset(ccarry[:], 0.0), writes=[ccarryb])
        CT = [s.sbuf(es, [128, 2, 512], F32, f"CT{h}") for h in range(NH)]
        CTb = [s.sbuf(es, [128, 2, 512], BF16, f"CTb{h}") for h in range(NH)]
        nst = [s.sbuf(es, [128, 2], F32, f"nst{h}") for h in range(NH)]
        nstb = [s.sbuf(es, [128, 2], BF16, f"nstb{h}") for h in range(NH)]
        mst = [s.sbuf(es, [128, 1], F32, f"mst{h}") for h in range(NH)]
        for h in range(NH):
            for t_ in (CT[h], CTb[h], nst[h], nstb[h], mst[h]):
                s.op("dve", lambda e, t_=t_: e.memset(t_[0][:], 0.0), writes=[t_[1]])
        Mrow = [s.sbuf(es, [128, 128], F32, f"Mrow{i}") for i in range(2)]
        DT = [s.sbuf(es, [128, 128], F32, f"DT{i}") for i in range(2)]
        wint = [s.sbuf(es, [128, 128], F32, f"wint{i}") for i in range(2)]
        small = [s.sbuf(es, [128, 128], F32, f"sm{i}") for i in range(2)]
        qw = [s.sbuf(es, [128, 2, 128], BF16, f"qw{i}") for i in range(2)]
        sc_sb = [s.sbuf(es, [128, 128], BF16, f"mscsb{i}") for i in range(2)]
        cols = [s.sbuf(es, [128, 16], F32, f"mcols{i}") for i in range(2)]
        wkb = [s.sbuf(es, [128, 1], BF16, f"wkb{i}") for i in range(2)]
        vw = [s.sbuf(es, [128, 512], BF16, f"vw{i}") for i in range(2)]
        hn = [s.sbuf(es, [128, 512], F32, f"hn{i}") for i in range(2)]
        junk = s.sbuf(es, [128, 512], BF16, "mjunk")
        gated = [s.sbuf(es, [128, 512], BF16, f"mgated{i}") for i in range(2)]
        oblk = s.sbuf(es, [128, NH * 4, TT], BF16, "moblk")
        proj_ps = cx.ps[0:3]
        b3t, b3b = cx.ps[3]
        sc_ps = (b3t[:, 0:128], b3b)
        den_ps = (b3t[:, 128:129], b3b)
        n_ps = (b3t[:, 130:132], b3b)
        b4t, b4b = cx.ps[4]
        b4bf = b4t[:].bitcast(BF16)
        kT_ps = (b4bf[:, 0:256], b4b)
        oT_ps = (b4bf[:, 256:768], b4b)
        num_ps = cx.ps[5]
        C_ps = [cx.ps[6], cx.ps[7]]
        wsrc = w_in.rearrange("(k p) n -> p k n", p=128)
        st8 = {"pp": 0}
        wtiles = [(tb, hl, kind) for tb in range(NBLK) for hl in range(NH) for kind in range(3)]

        def wload(idx):
            if idx >= len(wtiles):
                return
            tb, hl, kind = wtiles[idx]
            wt, wb = wring[idx % 4]
            c0 = hl * 1536 + kind * 512
            load_weight_bf16(s, wt, wb, wsrc[:, :, c0:c0 + 512], DC)

        def next_ps():
            p = proj_ps[st8["pp"] % 3]
            st8["pp"] += 1
            return p

        def mk_proj(tb, hl):
            widx = (tb * NH + hl) * 3
            par = (tb * NH + hl) % 2
            qkf, qkfb = qk_fm[par]
            vt, vtb = v_tm[par]
            ot, otb = o_tm[par]
            lit, lib = li15[par]
            nbt, nbb = nbr[par]
            crt, crb = crow[par]

            def qk_chunk(a, half):
                wt, wb = wring[widx % 4]
                idx = (hl * 2 + a) * 2 + half
                pp, ppb = next_ps()
                c0 = a * 256 + half * 128
                for k in range(DC):
                    s.mm(ppb, pp[:, :], wt[:, k, c0:c0 + 128], ht[:, k, :], start=(k == 0), stop=(k == DC - 1),
                         reads=[wb, hb])
                x, xb = xp[idx % 2]
                ac, acb = acc[idx % 2]
                s.copy("act", x[:, 3:TT + 3], pp[:, :], reads=[ppb], writes=[xb])
                s.copy("act", x[:, 0:3], ccarry[:, idx, :], reads=[ccarryb], writes=[xb])
                s.copy("act", ccarry[:, idx, :], x[:, TT:TT + 3], reads=[xb], writes=[ccarryb])
                s.ts(ac[:], x[:, 0:TT], cw[:, idx * 4:idx * 4 + 1], cbt[:, idx:idx + 1], ALU.mult, ALU.add,
                     reads=[xb, cwb, cbb], writes=[acb])
                for tap in (1, 2, 3):
                    s.stt(ac[:], x[:, tap:TT + tap], cw[:, idx * 4 + tap:idx * 4 + tap + 1], ac[:], ALU.mult, ALU.add,
                          reads=[xb, cwb, acb], writes=[acb])
                s.act(qkf[:, a, half, :], ac[:], AF.Silu, reads=[acb], writes=[qkfb])

            def gates():
                gi, gib = next_ps()
                for k in range(DC):
                    s.mm(gib, gi[:, :], wgr[:, k, hl, :], ht[:, k, :], start=(k == 0), stop=(k == DC - 1),
                         reads=[wgrb, hb])
                s.act(lit[:], gi[:, :], AF.Tanh, reads=[gib, gb15b], writes=[lib], scale=1.0 / GATE_CAP,
                      bias=gb15[:, hl:hl + 1])
                gf, gfb = next_ps()
                for k in range(DC):
                    s.mm(gfb, gf[:, :], wgr[:, k, 2 + hl, :], ht[:, k, :], start=(k == 0), stop=(k == DC - 1),
                         reads=[wgrb, hb])
                g1, g1b = gtmp[0]
                g2, g2b = gtmp[1]
                s.act(g1[:], gf[:, :], AF.Tanh, reads=[gfb, gb15b], writes=[g1b], scale=1.0 / GATE_CAP,
                      bias=gb15[:, 2 + hl:3 + hl])
                s.act(g2[:], g1[:], AF.Exp, reads=[g1b], writes=[g2b], scale=-GATE_CAP)
                s.act(g1[:], g2[:], AF.Ln, reads=[g2b, oneb[1]], writes=[g1b], bias=oneb[0][:], scale=1.0)
                for c4 in range(CPB):
                    cs_ = slice(c4 * 128, (c4 + 1) * 128)
                    s.op("dve", lambda e, cs_=cs_: e.tensor_tensor_scan(nbt[:, cs_], ones_r[0][:], g1[:, cs_], 0.0,
                                                                        ALU.mult, ALU.add),
                         reads=[ones_r[1], g1b], writes=[nbb])
                s.stt(crt[:], lit[:], GATE_CAP, nbt[:], ALU.mult, ALU.add, reads=[lib, nbb], writes=[crb])

            def p_q():
                wload(widx + 2)
                qk_chunk(0, 0)
                qk_chunk(0, 1)
                gates()

            def p_k():
                qk_chunk(1, 0)
                qk_chunk(1, 1)

            def tm_proj(kind):
                wt, wb = wring[(widx + kind) % 4]
                for c4 in range(CPB):
                    pp, ppb = next_ps()
                    for k in range(DC):
                        s.mm(ppb, pp[:, :], ht[:, k, c4 * 128:(c4 + 1) * 128], wt[:, k, :], start=(k == 0),
                             stop=(k == DC - 1), reads=[wb, hb])
                    if kind == 1:
                        s.copy("act", vt[:, c4, :], pp[:, :], reads=[ppb], writes=[vtb])
                    else:
                        s.act(ot[:, c4, :], pp[:, :], AF.Sigmoid, reads=[ppb], writes=[otb])

            def p_v():
                wload(widx + 3)
                tm_proj(1)

            def p_o():
                wload(widx + 4)
                tm_proj(2)

            return [p_q, p_k, p_v, p_o]

        def mk_recur(tb, hl):
            par = (tb * NH + hl) % 2
            qkf, qkfb = qk_fm[par]
            kt, ktb = k_tm[par]
            vt, vtb = v_tm[par]
            ot, otb = o_tm[par]
            nbt, nbb = nbr[par]
            crt, crb = crow[par]
            Ct, Cb = CT[hl]
            Cbt, Cbb = CTb[hl]
            nt, nb_ = nst[hl]
            nbt16, nbb16 = nstb[hl]
            mt_, mb_ = mst[hl]
            ob, obb = oblk

            def chunk(c4):
                i2 = (tb * NH * CPB + hl * CPB + c4) % 2
                cs_ = slice(c4 * 128, (c4 + 1) * 128)
                Mr, Mrb = Mrow[i2]
                Dt, Dtb = DT[i2]
                wi, wib = wint[i2]
                sm, smb = small[i2]
                cl, clb = cols[i2]
                for half in range(2):
                    s.transpose(kT_ps[1], kT_ps[0][:, half * 128:(half + 1) * 128], qkf[:, 1, half, cs_], ident[:],
                                reads=[qkfb, identb])
                s.copy("act", kt[:, c4, :], kT_ps[0], reads=[kT_ps[1]], writes=[ktb])
                s.op("dve", lambda e: e.tensor_tensor_scan(Mr[:], crt[:, cs_], crt[:, cs_], mt_[:, 0:1], ALU.max, ALU.max),
                     reads=[crb, mb_], writes=[Mrb])
                s.op("dve", lambda e: e.scalar_tensor_tensor(sm[:], crt[:, cs_], 1.0, identf[:], ALU.mult, ALU.mult,
                                                             accum_out=cl[:, 0:1]),
                     reads=[crb, identfb], writes=[smb, clb])
                s.tt(Dt[:], nbt[:, cs_], Mr[:], ALU.subtract, reads=[nbb, Mrb], writes=[Dtb])
                s.op("dve", lambda e: e.scalar_tensor_tensor(sm[:], Dt[:], 1.0, identf[:], ALU.mult, ALU.mult,
                                                             accum_out=cl[:, 1:2]),
                     reads=[Dtb, identfb], writes=[smb, clb])
                s.act(cl[:, 2:3], cl[:, 1:2], AF.Exp, reads=[clb], writes=[clb])
                s.ts(Dt[:], Mr[:], cl[:, 0:1], 0.0, ALU.subtract, ALU.max, reads=[Mrb, clb], writes=[Dtb])
                s.act(Dt[:], Dt[:], AF.Exp, reads=[Dtb], writes=[Dtb], scale=-1.0)
                s.tt(Dt[:], Dt[:], mask[:], ALU.mult, reads=[Dtb, maskb], writes=[Dtb])
                s.act(wi[:], Mr[:], AF.Exp, reads=[Mrb, mb_], writes=[wib], scale=-1.0, bias=mt_[:, 0:1])
                qwt, qwb = qw[i2]
                s.tt(qwt[:], qkf[:, 0, :, cs_], wi[:].unsqueeze(1).to_broadcast([128, 2, 128]), ALU.mult,
                     reads=[qkfb, wib], writes=[qwb])
                for half in range(2):
                    s.mm(sc_ps[1], sc_ps[0], qkf[:, 1, half, cs_], qkf[:, 0, half, cs_], start=(half == 0),
                         stop=(half == 1), reads=[qkfb])
                scs, scsb = sc_sb[i2]
                s.tt(scs[:], sc_ps[0], Dt[:], ALU.mult, reads=[sc_ps[1], Dtb], writes=[scsb])
                npt, npb = num_ps
                s.mm(npb, npt[:, :], scs[:], vt[:, c4, :], start=True, stop=False, reads=[scsb, vtb], sig=True)
                for half in range(2):
                    s.mm(npb, npt[:, :], qwt[:, half, :], Cbt[:, half, :], start=False, stop=(half == 1),
                         reads=[qwb, Cbb], sig=True)
                s.mm(den_ps[1], den_ps[0], scs[:], onec[0][:], start=True, stop=False, reads=[scsb, onec[1]], sig=True)
                for half in range(2):
                    s.mm(den_ps[1], den_ps[0], qwt[:, half, :], nbt16[:, half:half + 1], start=False, stop=(half == 1),
                         reads=[qwb, nbb16], sig=True)
                s.ts(cl[:, 3:4], den_ps[0], -1.0, None, ALU.mult, None, reads=[den_ps[1]], writes=[clb])
                s.tt(cl[:, 3:4], cl[:, 3:4], den_ps[0], ALU.max, reads=[den_ps[1], clb], writes=[clb])
                s.ts(cl[:, 3:4], cl[:, 3:4], cl[:, 2:3], None, ALU.max, None, reads=[clb], writes=[clb])
                s.op("dve", lambda e: e.reciprocal(cl[:, 3:4], cl[:, 3:4]), reads=[clb], writes=[clb])
                hnt, hnb = hn[i2]
                s.act(hnt[:], npt[:, :], AF.Identity, reads=[npb, clb], writes=[hnb], scale=cl[:, 3:4])
                s.act(junk[0][:], hnt[:], AF.Square, reads=[hnb], writes=[junk[1], clb], accum_out=cl[:, 4:5])
                s.act(cl[:, 5:6], cl[:, 4:5], AF.Sqrt, reads=[clb, epsb[1]], writes=[clb], scale=1.0 / 512, bias=epsb[0][:])
                s.op("dve", lambda e: e.reciprocal(cl[:, 5:6], cl[:, 5:6]), reads=[clb], writes=[clb])
                s.stt(hnt[:], hnt[:], cl[:, 5:6], ng[:, hl, :], ALU.mult, ALU.mult, reads=[hnb, clb, ngb], writes=[hnb])
                ga, gab = gated[i2]
                s.tt(ga[:], hnt[:], ot[:, c4, :], ALU.mult, reads=[hnb, otb], writes=[gab])
                for ec in range(4):
                    s.transpose(oT_ps[1], oT_ps[0][:, ec * 128:(ec + 1) * 128], ga[:, ec * 128:(ec + 1) * 128], ident[:],
                                reads=[gab, identb])
                s.copy("act", ob[:, hl * 4:(hl + 1) * 4, cs_], oT_ps[0].rearrange("p (e t) -> p e t", e=4),
                       reads=[oT_ps[1]], writes=[obb])
                s.ts(cl[:, 6:7], Mr[:, 127:128], -1.0, -float(np.log(16.0)), ALU.mult, ALU.add, reads=[Mrb], writes=[clb])
                s.act(cl[:, 7:8], cl[:, 0:1], AF.Exp, reads=[clb], writes=[clb], bias=cl[:, 6:7], scale=1.0)
                s.act(cl[:, 8:9], Mr[:, 127:128], AF.Exp, reads=[Mrb, mb_, clb], writes=[clb], scale=-1.0, bias=mt_[:, 0:1])
                wk, wkbb = wkb[i2]
                s.copy("dve", wk[:], cl[:, 7:8], reads=[clb], writes=[wkbb])
                vwt, vwb = vw[i2]
                s.ts(vwt[:], vt[:, c4, :], cl[:, 7:8], None, ALU.mult, None, reads=[vtb, clb], writes=[vwb])
                for half in range(2):
                    cpt, cpb = C_ps[half]
                    s.mm(cpb, cpt[:, :], kt[:, c4, half * 128:(half + 1) * 128], vwt[:], start=True, stop=True,
                         reads=[ktb, vwb])
                for half in range(2):
                    s.mm(n_ps[1], n_ps[0][:, half:half + 1], kt[:, c4, half * 128:(half + 1) * 128], wk[:],
                         start=True, stop=True, reads=[ktb, wkbb])
                for half in range(2):
                    cpt, cpb = C_ps[half]
                    s.stt(Ct[:, half, :], Ct[:, half, :], cl[:, 8:9], cpt[:, :], ALU.mult, ALU.add,
                          reads=[Cb, clb, cpb], writes=[Cb])
                s.stt(nt[:], nt[:], cl[:, 8:9], n_ps[0], ALU.mult, ALU.add, reads=[nb_, clb, n_ps[1]], writes=[nb_])
                s.copy("act", Cbt[:], Ct[:], reads=[Cb], writes=[Cbb])
                s.copy("act", nbt16[:], nt[:], reads=[nb_], writes=[nbb16])
                s.tt(mt_[:], Mr[:, 127:128], nbt[:, c4 * 128 + 127:c4 * 128 + 128], ALU.subtract, reads=[Mrb, nbb],
                     writes=[mb_])

            return [lambda c4=c4: chunk(c4) for c4 in range(CPB)]

        def block_io_start(tb):
            j, t0 = tb // 4, (tb % 4) * TT
            dma_c(s, "sp", ht[:], h_full[j, :, :, t0:t0 + TT].rearrange("c p t -> p c t"), DC, reads=[cx.dbuf("h_full")],
                  writes=[hb])

        def block_io_end(tb):
            j, t0 = tb // 4, (tb % 4) * TT
            ob, obb = oblk
            s.dma("act", o_out[j, :, :, t0:t0 + TT].rearrange("c p t -> p c t"), ob[:], reads=[obb],
                  writes=[cx.dbuf("o_out")], is_output=True)

        wload(0)
        wload(1)
        prev = None
        for tb in range(NBLK):
            for hl in range(NH):
                pr = mk_proj(tb, hl)
                if hl == 0:
                    first = pr[0]

                    def p0(first=first, tb=tb):
                        block_io_start(tb)
                        first()
                    pr[0] = p0
                rc = prev if prev is not None else []
                interleave(pr, rc)
                if prev is not None and hl == 0 and tb > 0:
                    block_io_end(tb - 1)
                prev = mk_recur(tb, hl)
        interleave([], prev)
        block_io_end(NBLK - 1)


def build_mlstm_prog():
    nc = new_prog()
    h_full = dram(nc, "h_full", [2, DC, 128, NT], BF16, "ExternalInput")
    w_in = dram(nc, "w_in", [D, 3072], F32, "ExternalInput")
    wgate = dram(nc, "wgate", [D, 4], F32, "ExternalInput")
    cw = dram(nc, "cw", [128, 32], F32, "ExternalInput")
    cb = dram(nc, "cb", [128, 8], F32, "ExternalInput")
    gb = dram(nc, "gb", [128, 4], F32, "ExternalInput")
    ng = dram(nc, "ng", [128, 1024], F32, "ExternalInput")
    mask = dram(nc, "mask", [128, 128], F32, "ExternalInput")
    ident = dram(nc, "ident", [128, 128], BF16, "ExternalInput")
    identf = dram(nc, "identf", [128, 128], F32, "ExternalInput")
    o_out = dram(nc, "o_out", [2, 8, 128, NT], BF16, "ExternalOutput")
    with ExitStack() as es:
        cx = Ctx(nc, es)
        emit_mlstm(cx, h_full, w_in, wgate, cw, cb, gb, ng, mask, ident, identf, o_out)
        cx.s.finish()
    return nc


def host_mlstm_inputs(w_in, conv_w, conv_b, gate_b, norm_g, r):
    import ml_dtypes
    cols = []
    cwl = []
    cbl = []
    for hl in range(2):
        h = 2 * r + hl
        qs = slice(h * 256, (h + 1) * 256)
        ks = slice(1024 + h * 256, 1024 + (h + 1) * 256)
        cols += [w_in[:, qs], w_in[:, ks], w_in[:, 2048 + h * 512:2048 + (h + 1) * 512],
                 w_in[:, 4096 + h * 512:4096 + (h + 1) * 512]]
        for sl in (qs, ks):
            for half in range(2):
                c0 = sl.start + half * 128
                cwl.append(conv_w[:, c0:c0 + 128].T)
                cbl.append(conv_b[c0:c0 + 128][:, None])
    hs = [2 * r, 2 * r + 1]
    wgate = np.stack([w_in[:, 6144 + hs[0]], w_in[:, 6144 + hs[1]], w_in[:, 6148 + hs[0]], w_in[:, 6148 + hs[1]]], axis=1)
    gbv = np.array([gate_b[hs[0]], gate_b[hs[1]], gate_b[4 + hs[0]], gate_b[4 + hs[1]]], np.float32)
    ngv = np.concatenate([norm_g[hs[0] * 512:(hs[0] + 1) * 512], norm_g[hs[1] * 512:(hs[1] + 1) * 512]])
    idx = np.arange(128)
    return {
        "w_in": np.ascontiguousarray(np.concatenate(cols, axis=1)),
        "wgate": np.ascontiguousarray(wgate),
        "cw": np.ascontiguousarray(np.concatenate(cwl, axis=1)).astype(np.float32),
        "cb": np.ascontiguousarray(np.concatenate(cbl, axis=1)).astype(np.float32),
        "gb": np.ascontiguousarray(np.broadcast_to(gbv[None], (128, 4))),
        "ng": np.ascontiguousarray(np.broadcast_to(ngv[None], (128, 1024))).astype(np.float32),
        "mask": ((idx[:, None] <= idx[None, :]).astype(np.float32) / 16.0),
        "ident": np.eye(128, dtype=np.float32).astype(ml_dtypes.bfloat16),
        "identf": np.eye(128, dtype=np.float32),
    }


RW_C = float(np.exp(-0.5))
RW_EPS = 64e-5
RW_DBG = {"nblk": 8, "nchunk": 8, "steps": 7}
LCH = 64


def emit_rwkv(cx, h_full, w_in, vecs_in, w2_in, a2_in, g2_in, cst_in, ident_in, o_out):
    s = cx.s
    NCK = 8
    with ExitStack() as es:
        ident, identb = s.sbuf(es, [128, 128], BF16, "ident")
        cst, cstb = s.sbuf(es, [128, 1088], F32, "rcst")
        vecs, vecb = s.sbuf(es, [128, 84], F32, "rvecs")
        s.dma("sp", ident[:], ident_in, writes=[identb])
        s.dma("sp", cst[:], cst_in, writes=[cstb])
        s.dma("sp", vecs[:], vecs_in, writes=[vecb])
        identf = cst[:, 0:128]
        bones = cst[:, 128:256]
        bavg = cst[:, 256:384]
        mask_b = cst[0:64, 384:512]
        mask_nt = cst[0:64, 512:576]
        rmask = cst[:, 576:1088]
        MU, W0, A0, KK, KA, RK, LNW, LNB = 0, 28, 36, 44, 52, 60, 68, 76
        w2t, w2b = s.sbuf(es, [128, 1024], BF16, "w2t")
        a2t, a2b = s.sbuf(es, [128, 1024], BF16, "a2t")
        g2t, g2b = s.sbuf(es, [128, 2, 1024], BF16, "g2t")
        s.dma("pool", w2t[0:96, :], w2_in, writes=[w2b])
        s.dma("pool", a2t[0:96, :], a2_in, writes=[a2b])
        s.dma("pool", g2t[:], g2_in.rearrange("(k p) n -> p k n", p=128), writes=[g2b])
        epsb = s.sbuf(es, [128, 1], F32, "epsw")
        s.op("dve", lambda e: e.memset(epsb[0][:], RW_EPS), writes=[epsb[1]])

        ht, hb = s.sbuf(es, [128, DC, TT], BF16, "hW")
        wring = [s.sbuf(es, [128, DC, 448], BF16, f"wW{i}") for i in range(3)]
        ARz = [s.sbuf(es, [128, 2, NCK, TT], BF16, f"AR{i}") for i in range(2)]
        for i in range(2):
            s.op("dve", lambda e, i=i: e.memset(ARz[i][0][:], 0.0), writes=[ARz[i][1]])
        BK, BKb = s.sbuf(es, [128, 2, NCK, TT], BF16, "BK")
        Vb, Vbb = s.sbuf(es, [128, NCK, TT], BF16, "Vb")
        gfm, gfmb = s.sbuf(es, [128, NCK, TT], BF16, "gfm")
        PL, PLb = s.sbuf(es, [128, NCK, TT // LCH], F32, "PL")
        carry, carryb = s.sbuf(es, [128, 28], F32, "rcarry")
        s.op("dve", lambda e: e.memset(carry[:], 0.0), writes=[carryb])
        wl, wlb = s.sbuf(es, [128, TT], BF16, "wl")
        al, alb = s.sbuf(es, [128, TT], BF16, "al")
        glo, glob = s.sbuf(es, [128, 2, TT], BF16, "glo")
        xs = [s.sbuf(es, [128, TT + 1], F32, f"xs{i}") for i in range(2)]
        tmp = [s.sbuf(es, [128, TT], F32, f"rtmp{i}") for i in range(10)]
        ST, STb_ = s.sbuf(es, [128, NCK, 64], F32, "ST")
        STh, SThb = s.sbuf(es, [128, NCK, 64], BF16, "STh")
        s.op("dve", lambda e: e.memset(ST[:], 0.0), writes=[STb_])
        s.op("dve", lambda e: e.memset(STh[:], 0.0), writes=[SThb])
        tms = [s.sbuf(es, [64, 1024], BF16, f"tm{i}") for i in range(5)]
        for i in range(4):
            s.op("dve", lambda e, i=i: e.memset(tms[i][0][:], 0.0), writes=[tms[i][1]])
        tmplain = [s.sbuf(es, [64, 1024], BF16, f"tmpl{i}") for i in range(2)]
        ytm, ytmb = s.sbuf(es, [64, 16, 64], F32, "ytm")
        oblk, oblkb = s.sbuf(es, [128, NCK, TT], BF16, "roblk")
        bonus, bonusb = oblk, oblkb
        grp = []
        for gi in range(2):
            d = {}
            for nm, shp in (("NMb", [64, 4, 128]), ("NMk", [64, 4, 128]), ("NT", [64, 4, 64]), ("Pa", [64, 4, 64]),
                            ("Pb", [64, 4, 64]), ("PTa", [64, 4, 64]), ("PTb", [64, 4, 64]), ("Xa", [64, 4, 64]),
                            ("Xb", [64, 4, 64]), ("XTs", [64, 4, 64]), ("UTs", [64, 4, 64])):
                d[nm] = s.sbuf(es, shp, BF16, f"{nm}{gi}")
            grp.append(d)
        wsrc = w_in.rearrange("(k p) n -> p k n", p=128)
        NTILE = 9
        st8 = {"pp": 0}

        def wload(idx):
            if idx >= NBLK * NTILE:
                return
            tb, j = divmod(idx, NTILE)
            wt, wb = wring[idx % 3]
            if j == 0:
                load_weight_bf16(s, wt, wb, wsrc[:, :, 3072:3520], DC)
            else:
                c = j - 1
                load_weight_bf16(s, wt[:, :, 0:384], wb, wsrc[:, :, c * 384:(c + 1) * 384], DC)

        def next_ps():
            p = cx.ps[st8["pp"] % 2]
            st8["pp"] += 1
            return p

        def proj_shift(wt, wb, c0, m, idx, out_ap, out_b, func=None, n_out_rows=128):
            pp, ppb = next_ps()
            for k in range(DC):
                s.mm(ppb, pp[0:m, :], wt[:, k, c0:c0 + m], ht[:, k, :], start=(k == 0), stop=(k == DC - 1),
                     reads=[wb, hb])
            x, xb = xs[idx % 2]
            s.copy("act", x[0:m, 1:TT + 1], pp[0:m, :], reads=[ppb], writes=[xb])
            s.copy("act", x[0:m, 0:1], carry[0:m, idx:idx + 1], reads=[carryb], writes=[xb])
            s.copy("act", carry[0:m, idx:idx + 1], x[0:m, TT:TT + 1], reads=[xb], writes=[carryb])
            d, db = tmp[9]
            s.tt(d[0:m, :], x[0:m, 0:TT], x[0:m, 1:TT + 1], ALU.subtract, reads=[xb], writes=[db])
            if func is None:
                s.stt(out_ap, d[0:m, :], vecs[0:m, MU + idx:MU + idx + 1], x[0:m, 1:TT + 1], ALU.mult, ALU.add,
                      reads=[db, vecb, xb], writes=[out_b])
            else:
                s.stt(d[0:m, :], d[0:m, :], vecs[0:m, MU + idx:MU + idx + 1], x[0:m, 1:TT + 1], ALU.mult, ALU.add,
                      reads=[db, vecb, xb], writes=[db])
                s.act(out_ap, d[0:m, :], func, reads=[db], writes=[out_b])

        def prep_block(tb):
            j, t0 = tb // 4, (tb % 4) * TT
            widx = tb * NTILE
            dma_c(s, "sp", ht[:], h_full[j, :, :, t0:t0 + TT].rearrange("c p t -> p c t"), DC, reads=[cx.dbuf("h_full")],
                  writes=[hb])
            if tb == 0:
                wload(0)
                wload(1)
            wload(widx + 2)
            wt, wb = wring[widx % 3]
            proj_shift(wt, wb, 0, 96, 24, wl[0:96, :], wlb, func=AF.Tanh)
            proj_shift(wt, wb, 96, 96, 25, al[0:96, :], alb, func=AF.Copy)
            proj_shift(wt, wb, 192, 128, 26, glo[:, 0, :], glob, func=AF.Sigmoid)
            proj_shift(wt, wb, 320, 128, 27, glo[:, 1, :], glob, func=AF.Sigmoid)
            for c in range(NCK):
                wload(widx + 3 + c)
                wt, wb = wring[(widx + 1 + c) % 3]
                rr, rrb = tmp[0]
                kk_, kkb = tmp[1]
                vv, vvb = tmp[2]
                proj_shift(wt, wb, 0, 128, c, rr[:], rrb)
                proj_shift(wt, wb, 128, 128, 8 + c, kk_[:], kkb)
                proj_shift(wt, wb, 256, 128, 16 + c, vv[:], vvb)
                s.copy("act", Vb[:, c, :], vv[:], reads=[vvb], writes=[Vbb])
                cs_ = slice(c * 128, (c + 1) * 128)
                pz, pzb = cx.ps[2]
                s.mm(pzb, pz[:, :], w2t[0:96, cs_], wl[0:96, :], start=True, stop=True, reads=[w2b, wlb])
                sg, sgb = tmp[3]
                s.act(sg[:], pz[:, :], AF.Sigmoid, reads=[pzb, vecb], writes=[sgb], bias=vecs[:, W0 + c:W0 + c + 1],
                      scale=1.0)
                pa, pab = cx.ps[3]
                s.mm(pab, pa[:, :], a2t[0:96, cs_], al[0:96, :], start=True, stop=True, reads=[a2b, alb])
                aa, aab = tmp[4]
                s.act(aa[:], pa[:, :], AF.Sigmoid, reads=[pab, vecb], writes=[aab], bias=vecs[:, A0 + c:A0 + c + 1],
                      scale=1.0)
                pg, pgb = cx.ps[2]
                for k2 in range(2):
                    s.mm(pgb, pg[:, :], g2t[:, k2, cs_], glo[:, k2, :], start=(k2 == 0), stop=(k2 == 1),
                         reads=[g2b, glob])
                s.copy("act", gfm[:, c, :], pg[:, :], reads=[pgb], writes=[gfmb])
                kr, krb = tmp[5]
                s.ts(kr[:], kk_[:], vecs[:, KK + c:KK + c + 1], None, ALU.mult, None, reads=[kkb, vecb], writes=[krb])
                sq, sqb = tmp[6]
                s.act(sq[:], kr[:], AF.Square, reads=[krb], writes=[sqb])
                pq, pqb = cx.ps[3]
                s.mm(pqb, pq[:, :], bones, sq[:], start=True, stop=True, reads=[cstb, sqb])
                s.act(sq[:], pq[:, :], AF.Sqrt, reads=[pqb], writes=[sqb])
                s.ts(sq[:], sq[:], 1e-12, None, ALU.max, None, reads=[sqb], writes=[sqb])
                s.op("dve", lambda e: e.reciprocal(sq[:], sq[:]), reads=[sqb], writes=[sqb])
                s.tt(kr[:], kr[:], sq[:], ALU.mult, reads=[krb, sqb], writes=[krb])
                t7, t7b = tmp[7]
                s.ts(t7[:], aa[:], 1.0, vecs[:, KA + c:KA + c + 1], ALU.subtract, ALU.mult, reads=[aab, vecb], writes=[t7b])
                s.stt(kk_[:], t7[:], 1.0, kk_[:], ALU.add, ALU.mult, reads=[t7b, kkb], writes=[kkb])
                s.tt(aa[:], kr[:], aa[:], ALU.mult, reads=[krb, aab], writes=[aab])
                s.stt(t7[:], kk_[:], vecs[:, RK + c:RK + c + 1], rr[:], ALU.mult, ALU.mult, reads=[kkb, vecb, rrb],
                      writes=[t7b])
                pr_, prb = cx.ps[2]
                s.mm(prb, pr_[:, :], bones, t7[:], start=True, stop=True, reads=[cstb, t7b])
                s.tt(bonus[:, c, :], pr_[:, :], vv[:], ALU.mult, reads=[prb, vvb], writes=[bonusb])
                cum, cumb = tmp[8]
                s.op("dve", lambda e, cum=cum, sg=sg: e.tensor_tensor_scan(cum[:], rmask, sg[:], 0.0, ALU.mult, ALU.add),
                     reads=[cstb, sgb], writes=[cumb])
                s.tt(sg[:], cum[:], sg[:], ALU.subtract, reads=[cumb, sgb], writes=[sgb])
                s.act(sg[:], sg[:], AF.Exp, reads=[sgb], writes=[sgb], scale=-RW_C)
                for hf in range(2):
                    ps_ = slice(hf * 64, (hf + 1) * 64)
                    s.stt(ARz[hf][0][ps_, 0, c, :], kr[ps_, :], -1.0, sg[ps_, :], ALU.mult, ALU.mult, reads=[krb, sgb],
                          writes=[ARz[hf][1]])
                s.act(sg[:], cum[:], AF.Exp, reads=[cumb], writes=[sgb], scale=-RW_C)
                for hf in range(2):
                    ps_ = slice(hf * 64, (hf + 1) * 64)
                    s.tt(ARz[hf][0][ps_, 1, c, :], rr[ps_, :], sg[ps_, :], ALU.mult, reads=[rrb, sgb],
                         writes=[ARz[hf][1]])
                s.copy("act", PL[:, c, :], sg[:].rearrange("p (n l) -> p n l", l=LCH)[:, :, LCH - 1], reads=[sgb],
                       writes=[PLb])
                s.act(sg[:], cum[:], AF.Exp, reads=[cumb], writes=[sgb], scale=RW_C)
                s.tt(BK[:, 0, c, :], aa[:], sg[:], ALU.mult, reads=[aab, sgb], writes=[BKb])
                s.tt(BK[:, 1, c, :], kk_[:], sg[:], ALU.mult, reads=[kkb, sgb], writes=[BKb])

        def group_steps(ci, g, d, banks):
            tok = slice(ci * LCH, (ci + 1) * LCH)
            (pA, pAb), (pB, pBb), (pC, pCb) = banks
            vA = pA[0:64, :].rearrange("p (h n) -> p h n", h=4)
            vB = pB[0:64, :].rearrange("p (h n) -> p h n", h=4)
            vC = pC[0:64, 0:256].rearrange("p (h n) -> p h n", h=4)
            vA64 = pA[0:64, 0:256].rearrange("p (h n) -> p h n", h=4)
            vB64 = pB[0:64, 0:256].rearrange("p (h n) -> p h n", h=4)
            heads = [(4 * g + i, (4 * g + i) // 2, (4 * g + i) % 2) for i in range(4)]
            NMb, NMbb = d["NMb"]
            NMk, NMkb = d["NMk"]
            NT, NTb = d["NT"]
            XTs, XTsb = d["XTs"]
            UTs, UTsb = d["UTs"]
            btmz, ktmz, vtm = (tms[0], tms[1]), (tms[2], tms[3]), tms[4]
            steps = []

            def s_blocks():
                for i, (hd, c, hf) in enumerate(heads):
                    s.mm(pAb, vA[:, i, :], BK[:, 0, c, tok], ARz[hf][0][:, :, c, tok], start=True, stop=True,
                         reads=[BKb, ARz[hf][1]], sig=(i == 3))
                for i, (hd, c, hf) in enumerate(heads):
                    s.mm(pBb, vB[:, i, :], BK[:, 1, c, tok], ARz[hf][0][:, :, c, tok], start=True, stop=True,
                         reads=[BKb, ARz[hf][1]], sig=(i == 3))
                for i, (hd, c, hf) in enumerate(heads):
                    s.mm(pCb, vC[:, i, :], ARz[hf][0][:, 0, c, tok], BK[:, 0, c, tok], start=True, stop=True,
                         reads=[BKb, ARz[hf][1]], sig=(i == 3))
                mb4 = mask_b.unsqueeze(1).to_broadcast([64, 4, 128])
                s.tt(NMb[:], vA, mb4, ALU.mult, reads=[pAb, cstb], writes=[NMbb])
                s.tt(NMk[:], vB, mb4, ALU.mult, reads=[pBb, cstb], writes=[NMkb])
                s.tt(NT[:], vC, mask_nt.unsqueeze(1).to_broadcast([64, 4, 64]), ALU.mult, reads=[pCb, cstb],
                     writes=[NTb])
                X0, X0b = d["Xa"]
                s.tt(X0[:], NMb[:, :, 0:64], ident[0:64, 0:64].unsqueeze(1).to_broadcast([64, 4, 64]), ALU.add,
                     reads=[NMbb, identb], writes=[X0b])
            steps.append(s_blocks)

            def mk_level(k):
                def lvl():
                    if k == 1:
                        Pp, Ppb = NMb, NMbb
                        Pp_ap = lambda i: NMb[:, i, 0:64]
                        PTp, PTpb = NT, NTb
                    else:
                        Pp, Ppb = d["Pa" if k % 2 == 0 else "Pb"]
                        Pp_ap = lambda i, Pp=Pp: Pp[:, i, :]
                        PTp, PTpb = d["PTa" if k % 2 == 0 else "PTb"]
                    Pc, Pcb = d["Pb" if k % 2 == 0 else "Pa"]
                    PTc, PTcb = d["PTb" if k % 2 == 0 else "PTa"]
                    Xp, Xpb = d["Xa" if k % 2 == 1 else "Xb"]
                    Xc, Xcb = d["Xb" if k % 2 == 1 else "Xa"]
                    if k <= 4:
                        for i in range(4):
                            s.mm(pAb, vA64[:, i, :], PTp[:, i, :], Pp_ap(i), start=True, stop=True, reads=[PTpb, Ppb],
                                 sig=(i == 3))
                    for i in range(4):
                        s.mm(pBb, vB64[:, i, :], Pp_ap(i), PTp[:, i, :], start=True, stop=True, reads=[PTpb, Ppb],
                             sig=(i == 3))
                    if k <= 4:
                        s.copy("act", Pc[:], vA64, reads=[pAb], writes=[Pcb])
                    s.copy("dve", PTc[:], vB64, reads=[pBb], writes=[PTcb])
                    for i in range(4):
                        s.mm(pCb, vC[:, i, :], PTc[:, i, :], Xp[:, i, :], start=True, stop=True, reads=[PTcb, Xpb],
                             sig=(i == 3))
                    s.tt(Xc[:], vC, Xp[:], ALU.add, reads=[pCb, Xpb], writes=[Xcb])
                return lvl
            for k in range(1, 6):
                steps.append(mk_level(k))

            def s_apply():
                Xf, Xfb = d["Xb"]
                for i, (hd, c, hf) in enumerate(heads):
                    hc = slice(hd * 64, (hd + 1) * 64)
                    s.mm(pAb, vA64[:, i, :], ARz[hf][0][:, 0, c, tok], STh[:, c, :], start=True, stop=False,
                         reads=[ARz[hf][1], SThb], sig=False)
                    s.mm(pAb, vA64[:, i, :], NMk[:, i, 0:64], vtm[0][:, hc], start=False, stop=True,
                         reads=[NMkb, vtm[1]], sig=(i == 3))
                s.copy("act", XTs[:], vA64, reads=[pAb], writes=[XTsb])
                for i in range(4):
                    s.mm(pBb, vB64[:, i, :], Xf[:, i, :], XTs[:, i, :], start=True, stop=True, reads=[Xfb, XTsb],
                         sig=(i == 3))
                s.copy("dve", UTs[:], vB64, reads=[pBb], writes=[UTsb])
                for i, (hd, c, hf) in enumerate(heads):
                    hc = slice(hd * 64, (hd + 1) * 64)
                    s.mm(pCb, vC[:, i, :], ARz[hf][0][:, 1, c, tok], STh[:, c, :], start=True, stop=False,
                         reads=[ARz[hf][1], SThb], sig=False)
                    s.mm(pCb, vC[:, i, :], NMb[:, i, 64:128], UTs[:, i, :], start=False, stop=False,
                         reads=[NMbb, UTsb], sig=False)
                    s.mm(pCb, vC[:, i, :], NMk[:, i, 64:128], vtm[0][:, hc], start=False, stop=True,
                         reads=[NMkb, vtm[1]], sig=(i == 3))
                s.copy("act", ytm[:, 4 * g:4 * g + 4, :], vC, reads=[pCb], writes=[ytmb])
                p7, p7b = cx.ps[7]
                for cc in range(2):
                    c = 2 * g + cc
                    ccols = slice(c * 128, (c + 1) * 128)
                    for hf in range(2):
                        i = cc * 2 + hf
                        hc = slice((2 * c + hf) * 64, (2 * c + hf + 1) * 64)
                        s.mm(p7b, p7[:, c * 64:(c + 1) * 64], btmz[hf][0][:, ccols], UTs[:, i, :], start=(hf == 0),
                             stop=False, reads=[btmz[hf][1], UTsb], sig=False)
                        s.mm(p7b, p7[:, c * 64:(c + 1) * 64], ktmz[hf][0][:, ccols], vtm[0][:, hc], start=False,
                             stop=(hf == 1), reads=[ktmz[hf][1], vtm[1]], sig=(hf == 1))
            steps.append(s_apply)
            return steps

        def recur_chunk(tb, ci):
            tok = slice(ci * LCH, (ci + 1) * LCH)
            p3, p3b = cx.ps[3]
            p3bf = p3[:].bitcast(BF16)
            srcs = [(BK, BKb, 0), (BK, BKb, 1), (Vb, Vbb, None)]
            for q, (src, srcb, sel) in enumerate(srcs if RW_DBG.get("tm", 1) else []):
                if q not in RW_DBG.get("tmq", (0, 1, 2)):
                    continue
                for c in range(NCK):
                    in_ap = src[:, sel, c, tok] if sel is not None else src[:, c, tok]
                    s.transpose(p3b, p3bf[0:64, c * 128:(c + 1) * 128], in_ap, ident[:], reads=[srcb, identb])
                if q == 2:
                    s.copy("act", tms[4][0][:], p3bf[0:64, :], reads=[p3b], writes=[tms[4][1]])
                else:
                    pl_, plb_ = tmplain[q]
                    s.copy("act", pl_[:], p3bf[0:64, :], reads=[p3b], writes=[plb_])
                    pv = pl_[:].rearrange("p (c n) -> p c n", c=NCK)
                    for hf in range(2):
                        dst = tms[2 * q + hf][0][:].rearrange("p (c n) -> p c n", c=NCK)
                        s.copy("dve" if hf == 0 else "pool", dst[:, :, hf * 64:(hf + 1) * 64], pv[:, :, hf * 64:(hf + 1) * 64],
                               reads=[plb_], writes=[tms[2 * q + hf][1]])
            sets = [(cx.ps[0], cx.ps[1], cx.ps[2]), (cx.ps[4], cx.ps[5], cx.ps[6])]
            for pair in range(2):
                a = group_steps(ci, 2 * pair, grp[0], sets[0])
                b = group_steps(ci, 2 * pair + 1, grp[1], sets[1])
                nst_ = RW_DBG["steps"]
                interleave(a[:nst_], b[:nst_])
            p7, p7b = cx.ps[7]
            plb = PL[:, :, ci:ci + 1].to_broadcast([128, NCK, 64])
            t0_, t0b = tmp[0]
            inc = t0_[:].rearrange("p (c v) -> p c v", c=NCK)
            if not RW_DBG.get("state", 1):
                return
            s.tt(inc, p7[:, :].rearrange("p (c v) -> p c v", c=NCK), plb, ALU.mult, reads=[p7b, PLb], writes=[t0b])
            s.tt(ST[:], ST[:], plb, ALU.mult, reads=[STb_, PLb], writes=[STb_])
            s.tt(ST[:], ST[:], inc, ALU.add, reads=[STb_, t0b], writes=[STb_])
            s.copy("act", STh[:], ST[:], reads=[STb_], writes=[SThb])
            if not RW_DBG.get("yproc", 1):
                return
            yf, yfb = tmp[1]
            for c in range(NCK):
                s.transpose(p3b, p3[:, c * 64:(c + 1) * 64], ytm[:, 2 * c:2 * c + 2, :].rearrange("p h v -> p (h v)"),
                            identf[0:64, 0:64], reads=[ytmb, cstb])
            s.copy("act", yf[:], p3[:, :], reads=[p3b], writes=[yfb])
            sq, sqb = tmp[2]
            s.act(sq[:], yf[:], AF.Square, reads=[yfb], writes=[sqb])
            pm, pmb = cx.ps[0]
            pq, pqb = cx.ps[1]
            s.mm(pmb, pm[:, :], bavg, yf[:], start=True, stop=True, reads=[cstb, yfb])
            s.mm(pqb, pq[:, :], bavg, sq[:], start=True, stop=True, reads=[cstb, sqb])
            mn, mnb = tmp[3]
            s.copy("act", mn[:], pm[:, :], reads=[pmb], writes=[mnb])
            vr, vrb = tmp[4]
            s.tt(vr[:], mn[:], mn[:], ALU.mult, reads=[mnb], writes=[vrb])
            s.tt(vr[:], pq[:, :], vr[:], ALU.subtract, reads=[pqb, vrb], writes=[vrb])
            s.act(vr[:], vr[:], AF.Sqrt, reads=[vrb, epsb[1]], writes=[vrb], bias=epsb[0][:], scale=1.0)
            s.op("dve", lambda e: e.reciprocal(vr[:], vr[:]), reads=[vrb], writes=[vrb])
            s.tt(yf[:], yf[:], mn[:], ALU.subtract, reads=[yfb, mnb], writes=[yfb])
            s.tt(yf[:], yf[:], vr[:], ALU.mult, reads=[yfb, vrb], writes=[yfb])
            y3 = yf[:].rearrange("p (c t) -> p c t", c=NCK)
            s.tt(y3, y3, vecs[:, LNW:LNW + NCK].unsqueeze(2).to_broadcast([128, NCK, LCH]), ALU.mult, reads=[yfb, vecb],
                 writes=[yfb])
            s.tt(y3, y3, vecs[:, LNB:LNB + NCK].unsqueeze(2).to_broadcast([128, NCK, LCH]), ALU.add, reads=[yfb, vecb],
                 writes=[yfb])
            s.tt(y3, y3, bonus[:, :, tok], ALU.add, reads=[yfb, bonusb], writes=[yfb])
            s.tt(oblk[:, :, tok], y3, gfm[:, :, tok], ALU.mult, reads=[yfb, gfmb, oblkb], writes=[oblkb])

        for tb in range(RW_DBG["nblk"]):
            prep_block(tb)
            for ci in range(RW_DBG["nchunk"]):
                recur_chunk(tb, ci)
            j, t0 = tb // 4, (tb % 4) * TT
            s.dma("act", o_out[j, :, :, t0:t0 + TT].rearrange("c p t -> p c t"), oblk[:], reads=[oblkb],
                  writes=[cx.dbuf("o_out")], is_output=True)


def build_rwkv_prog():
    nc = new_prog()
    h_full = dram(nc, "h_full", [2, DC, 128, NT], BF16, "ExternalInput")
    w_in = dram(nc, "w_in", [D, 3520], F32, "ExternalInput")
    vecs = dram(nc, "vecs", [128, 84], F32, "ExternalInput")
    w2 = dram(nc, "w2", [96, 1024], F32, "ExternalInput")
    a2 = dram(nc, "a2", [96, 1024], F32, "ExternalInput")
    g2 = dram(nc, "g2", [256, 1024], F32, "ExternalInput")
    cst = dram(nc, "cst", [128, 1088], F32, "ExternalInput")
    ident = dram(nc, "ident", [128, 128], BF16, "ExternalInput")
    o_out = dram(nc, "o_out", [2, 8, 128, NT], BF16, "ExternalOutput")
    with ExitStack() as es:
        cx = Ctx(nc, es)
        emit_rwkv(cx, h_full, w_in, vecs, w2, a2, g2, cst, ident, o_out)
        cx.s.finish()
    return nc


def host_rwkv_inputs(P, r):
    import ml_dtypes
    own = slice(1024 * r, 1024 * (r + 1))
    w_in = P["w_in"]
    mu = P["mu"]
    cols, mus = [], []
    for c in range(8):
        for base in (0, 2048, 4096):
            sl = slice(base + 1024 * r + c * 128, base + 1024 * r + (c + 1) * 128)
            cols.append(w_in[:, sl])
    cols.append(w_in[:, 6144:6592])
    w_core = np.ascontiguousarray(np.concatenate(cols, axis=1))
    mu_cols = np.zeros((128, 28), np.float32)
    for c in range(8):
        for q, base in enumerate((0, 2048, 4096)):
            mu_cols[:, q * 8 + c] = mu[base + 1024 * r + c * 128: base + 1024 * r + (c + 1) * 128]
    mu_cols[0:96, 24] = mu[6144:6240]
    mu_cols[0:96, 25] = mu[6240:6336]
    mu_cols[:, 26] = mu[6336:6464]
    mu_cols[:, 27] = mu[6464:6592]
    vl = [mu_cols]
    for nm in ("w0", "a0", "k_k", "k_a", "r_k", "ln_w", "ln_b"):
        vl.append(vec_cols(np.asarray(P[nm]).reshape(-1)[own]))
    vecs = np.ascontiguousarray(np.concatenate(vl, axis=1)).astype(np.float32)
    idx = np.arange(128)
    blk = (idx[:, None] // 64 == idx[None, :] // 64).astype(np.float32)
    i64 = np.arange(64)
    mask_b = np.zeros((128, 128), np.float32)
    mask_b[0:64, 0:64] = (i64[:, None] < i64[None, :])
    mask_b[0:64, 64:128] = (i64[:, None] <= i64[None, :])
    mask_nt = np.zeros((128, 64), np.float32)
    mask_nt[0:64] = (i64[:, None] > i64[None, :])
    rmask = np.ones((128, 512), np.float32)
    rmask[:, ::64] = 0.0
    cst = np.concatenate([np.eye(128, dtype=np.float32), blk, blk / 64.0, mask_b, mask_nt, rmask], axis=1)
    return {"w_in": w_core, "vecs": vecs, "w2": np.ascontiguousarray(P["w2"][:, own]),
            "a2": np.ascontiguousarray(P["a2"][:, own]), "g2": np.ascontiguousarray(P["g2"][:, own]),
            "cst": np.ascontiguousarray(cst), "ident": np.eye(128, dtype=np.float32).astype(ml_dtypes.bfloat16)}


_PROGS = {}


def _prog(key, fn, *a):
    if key not in _PROGS:
        _PROGS[key] = fn(*a)
    return _PROGS[key]


def kernel_unfused(x, c, mod_w, mod_b, norm_mix_g, norm_ffn_g, ret_w_in, ret_w_out, rwkv_w_in, rwkv_mu, rwkv_w0, rwkv_w2,
           rwkv_a0, rwkv_a2, rwkv_g2, rwkv_k_k, rwkv_k_a, rwkv_r_k, rwkv_ln_w, rwkv_ln_b, rwkv_w_out, mlstm_w_in,
           mlstm_conv_w, mlstm_conv_b, mlstm_gate_b, mlstm_norm_g, mlstm_w_out, ffn_w_up, ffn_conv_w, ffn_conv_b,
           ffn_w_down, final_g, final_mod_w, final_mod_b):
    import ml_dtypes
    f32 = lambda a: np.ascontiguousarray(np.asarray(a, dtype=np.float32))
    x = f32(x)
    cores = [(k // 2, k % 2) for k in range(NCORES)]
    res = run(_prog("mod", build_mod_prog), host_mod_inputs(f32(c), f32(mod_w), f32(mod_b), f32(final_mod_w), f32(final_mod_b)))
    modv = host_modv(res)
    gv = host_gv(f32(norm_mix_g), f32(norm_ffn_g), f32(final_g))
    x_fm = [to_fm(x[b, r * NT:(r + 1) * NT]) for (b, r) in cores]
    res = run(_prog("a0", build_a0_prog, 0), [{"x_fm": x_fm[k], "modv": modv[b], "gv": gv} for k, (b, r) in enumerate(cores)])
    h_own = [np.asarray(r_["h_out"]) for r_ in res]
    out_fm = None
    for layer in range(4):
        kind, j = layer % 3, layer // 3
        h_full = [np.ascontiguousarray(np.stack([h_own[2 * b], h_own[2 * b + 1]])) for (b, r) in cores]
        if kind == 0:
            prog = _prog("ret", build_ret_prog)
            w_in = f32(ret_w_in[j])
            maps = []
            for k, (b, r) in enumerate(cores):
                m = {"h_full": h_full[k], "w_in": host_ret_w(w_in, r)}
                m.update(host_ret_consts(r))
                maps.append(m)
            w_out = f32(ret_w_out[j])
            KC = 32
        elif kind == 1:
            prog = _prog("rwkv", build_rwkv_prog)
            P = {"w_in": f32(rwkv_w_in[j]), "mu": f32(rwkv_mu[j]), "w0": f32(rwkv_w0[j]), "w2": f32(rwkv_w2[j]),
                 "a0": f32(rwkv_a0[j]), "a2": f32(rwkv_a2[j]), "g2": f32(rwkv_g2[j]), "k_k": f32(rwkv_k_k[j]),
                 "k_a": f32(rwkv_k_a[j]), "r_k": f32(rwkv_r_k[j]), "ln_w": f32(rwkv_ln_w[j]), "ln_b": f32(rwkv_ln_b[j])}
            maps = []
            for k, (b, r) in enumerate(cores):
                m = {"h_full": h_full[k]}
                m.update(host_rwkv_inputs(P, r))
                maps.append(m)
            w_out = f32(rwkv_w_out[j])
            KC = 16
        else:
            prog = _prog("mlstm", build_mlstm_prog)
            maps = []
            for k, (b, r) in enumerate(cores):
                m = {"h_full": h_full[k]}
                m.update(host_mlstm_inputs(f32(mlstm_w_in[j]), f32(mlstm_conv_w[j]), f32(mlstm_conv_b[j]),
                                           f32(mlstm_gate_b[j]), f32(mlstm_norm_g[j]), r))
                maps.append(m)
            w_out = f32(mlstm_w_out[j])
            KC = 16
        res = run(prog, maps)
        o_all = [np.asarray(r_["o_out"]) for r_ in res]
        o_x = [np.ascontiguousarray(np.concatenate([o_all[2 * b][r], o_all[2 * b + 1][r]], axis=0)) for (b, r) in cores]
        res = run(_prog(("c1", layer, KC), build_c1_prog, layer, KC),
                  [{"x_fm": x_fm[k], "o_x": o_x[k], "w_out": w_out, "modv": modv[b], "gv": gv}
                   for k, (b, r) in enumerate(cores)])
        xmid = [np.asarray(r_["xmid"]) for r_ in res]
        hf = [np.asarray(r_["hf"]) for r_ in res]
        final = layer == 3
        cw, cb = host_conv(f32(ffn_conv_w[layer]), f32(ffn_conv_b[layer]))
        maps = []
        for k, (b, r) in enumerate(cores):
            if r == 0:
                halo = np.zeros((DC, 128, 2), ml_dtypes.bfloat16)
            else:
                halo = np.ascontiguousarray(hf[2 * b][:, :, NT - 2:NT])
            maps.append({"xmid": xmid[k], "hf": hf[k], "halo": halo, "w_up": f32(ffn_w_up[layer]), "conv_w": cw,
                         "conv_b": cb, "w_down": f32(ffn_w_down[layer]), "modv": modv[b], "gv": gv})
        res = run(_prog(("c2", layer), build_c2_prog, layer, final), maps)
        if final:
            out_fm = [np.asarray(r_["hnext"]) for r_ in res]
        else:
            x_fm = [np.asarray(r_["xnew"]) for r_ in res]
            h_own = [np.asarray(r_["hnext"]) for r_ in res]
    out = np.empty((B, T, D), np.float32)
    for k, (b, r) in enumerate(cores):
        out[b, r * NT:(r + 1) * NT] = from_fm(out_fm[k])
    return out


NFC = 4


def build_fused_prog():
    nc = new_prog()
    EI = "ExternalInput"
    x_in = dram(nc, "x_fm", [2, DC, 128, NT], F32, EI)
    c4 = dram(nc, "c4", [4, D], F32, EI)
    modw = dram(nc, "modw", [D, NMODCH * 128], F32, EI)
    modb = dram(nc, "modb", [NMODCH * 128], F32, EI)
    gv_in = dram(nc, "gv", [128, 9 * DC], F32, EI)
    zhalo = dram(nc, "zhalo", [DC, 128, 2], BF16, EI)
    cos_in = dram(nc, "cos", [128, T], F32, EI)
    sin_in = dram(nc, "sin", [128, T], F32, EI)
    mask_in = dram(nc, "mask", [128, 128], F32, EI)
    mask16_in = dram(nc, "mask16", [128, 128], F32, EI)
    ident_in = dram(nc, "ident", [128, 128], BF16, EI)
    identf_in = dram(nc, "identf", [128, 128], F32, EI)
    ret_rdec = dram(nc, "ret_rdec", [2, 128, 1024], F32, EI)
    ret_gl = dram(nc, "ret_gl", [2, 128, 4], F32, EI)
    ret_win = dram(nc, "ret_win", [2, 2, D, 6144], F32, EI)
    ret_wout = dram(nc, "ret_wout", [2, 4096, D], F32, EI)
    rw_win = dram(nc, "rw_win", [2, D, 3520], F32, EI)
    rw_vecs = dram(nc, "rw_vecs", [2, 128, 84], F32, EI)
    rw_w2 = dram(nc, "rw_w2", [2, 96, 1024], F32, EI)
    rw_a2 = dram(nc, "rw_a2", [2, 96, 1024], F32, EI)
    rw_g2 = dram(nc, "rw_g2", [2, 256, 1024], F32, EI)
    rw_cst = dram(nc, "rw_cst", [128, 1088], F32, EI)
    rw_wout = dram(nc, "rw_wout", [D, D], F32, EI)
    ml_win = dram(nc, "ml_win", [2, D, 3072], F32, EI)
    ml_wgate = dram(nc, "ml_wgate", [2, D, 4], F32, EI)
    ml_cw = dram(nc, "ml_cw", [2, 128, 32], F32, EI)
    ml_cb = dram(nc, "ml_cb", [2, 128, 8], F32, EI)
    ml_gb = dram(nc, "ml_gb", [2, 128, 4], F32, EI)
    ml_ng = dram(nc, "ml_ng", [2, 128, 1024], F32, EI)
    ml_wout = dram(nc, "ml_wout", [D, D], F32, EI)
    f_wup = dram(nc, "f_wup", [4, D, 2 * DFF], F32, EI)
    f_cw = dram(nc, "f_cw", [4, 128, 3 * FC], F32, EI)
    f_cb = dram(nc, "f_cb", [4, 128, FC], F32, EI)
    f_wdown = dram(nc, "f_wdown", [4, DFF, D], F32, EI)
    out = dram(nc, "out_fm", [2, DC, 128, NT], F32, "ExternalOutput")
    sX = dram(nc, "sX", [2, DC, 128, NT], F32, "Internal")
    sXM = dram(nc, "sXM", [2, DC, 128, NT], F32, "Internal")
    sH = dram(nc, "sH", [2, DC, 128, NT], BF16, "Internal")
    sHF = dram(nc, "sHF", [2, DC, 128, NT], BF16, "Internal")
    sO = dram(nc, "sO", [2, 32, 128, NT], BF16, "Internal")
    with ExitStack() as es:
        cx = Ctx(nc, es)
        s = cx.s
        mt, mb = s.sbuf(es, [128, NMODCH], F32, "modv")
        gt, gb = s.sbuf(es, [128, 9 * DC], F32, "gv")
        s.dma("sp", gt[:], gv_in, writes=[gb])
        modv, gv = (mt, mb), (gt, gb)
        n = MCH * 128
        for jj in range(NMODCH // MCH):
            emit_mod(cx, c4, modw[:, jj * n:(jj + 1) * n], modb[jj * n:(jj + 1) * n], None, modv_dst=(modv, jj * MCH))
            s.barrier()
        for r in range(2):
            emit_prenorm(cx, x_in[r], sH[r], modv, gv, 0, "sH")
            s.barrier()
        for layer in range(4):
            kind, j = layer % 3, layer // 3
            for r in range(2):
                if kind == 0:
                    emit_ret(cx, sH, ret_win[j, r], cos_in, sin_in, ret_rdec[r], ret_gl[r], mask_in, ident_in,
                             sO[:, r * 16:(r + 1) * 16])
                elif kind == 1:
                    emit_rwkv(cx, sH, rw_win[r], rw_vecs[r], rw_w2[r], rw_a2[r], rw_g2[r], rw_cst, ident_in,
                              sO[:, r * 8:(r + 1) * 8])
                else:
                    emit_mlstm(cx, sH, ml_win[r], ml_wgate[r], ml_cw[r], ml_cb[r], ml_gb[r], ml_ng[r], mask16_in,
                               ident_in, identf_in, sO[:, r * 8:(r + 1) * 8])
                s.barrier()
            KC = 32 if kind == 0 else 16
            w_out = ret_wout[j] if kind == 0 else (rw_wout if kind == 1 else ml_wout)
            xsrc = x_in if layer == 0 else sX
            for r in range(2):
                emit_c1(cx, xsrc[r], sO[r, 0:KC], w_out, sXM[r], sHF[r], modv, gv, layer, KC)
                s.barrier()
            final = layer == 3
            for r in range(2):
                halo = zhalo if r == 0 else sHF[0, :, :, NT - 2:NT]
                emit_c2(cx, sXM[r], sHF[r], halo, f_wup[layer], f_cw[layer], f_cb[layer], f_wdown[layer], sX[r],
                        out[r] if final else sH[r], modv, gv, layer, final)
                s.barrier()
        s.finish()
    return nc


def kernel(x, c, mod_w, mod_b, norm_mix_g, norm_ffn_g, ret_w_in, ret_w_out, rwkv_w_in, rwkv_mu, rwkv_w0, rwkv_w2,
           rwkv_a0, rwkv_a2, rwkv_g2, rwkv_k_k, rwkv_k_a, rwkv_r_k, rwkv_ln_w, rwkv_ln_b, rwkv_w_out, mlstm_w_in,
           mlstm_conv_w, mlstm_conv_b, mlstm_gate_b, mlstm_norm_g, mlstm_w_out, ffn_w_up, ffn_conv_w, ffn_conv_b,
           ffn_w_down, final_g, final_mod_w, final_mod_b):
    import ml_dtypes
    f32 = lambda a: np.ascontiguousarray(np.asarray(a, dtype=np.float32))
    x = f32(x)
    c = f32(c)
    shared = {}
    shared["modw"] = np.ascontiguousarray(np.concatenate([f32(mod_w[i]) for i in range(4)] + [f32(final_mod_w)], axis=1))
    shared["modb"] = np.ascontiguousarray(np.concatenate([f32(mod_b[i]) for i in range(4)] + [f32(final_mod_b)], axis=0))
    shared["gv"] = host_gv(f32(norm_mix_g), f32(norm_ffn_g), f32(final_g))
    shared["zhalo"] = np.zeros((DC, 128, 2), ml_dtypes.bfloat16)
    rc = [host_ret_consts(r) for r in range(2)]
    shared["cos"], shared["sin"], shared["mask"], shared["ident"] = rc[0]["cos"], rc[0]["sin"], rc[0]["mask"], rc[0]["ident"]
    shared["mask16"] = np.ascontiguousarray(rc[0]["mask"] / 16.0)
    shared["identf"] = np.eye(128, dtype=np.float32)
    shared["ret_rdec"] = np.stack([rc[0]["rdec"], rc[1]["rdec"]])
    shared["ret_gl"] = np.stack([rc[0]["gl"], rc[1]["gl"]])
    shared["ret_win"] = np.stack([np.stack([host_ret_w(f32(ret_w_in[j]), r) for r in range(2)]) for j in range(2)])
    shared["ret_wout"] = f32(ret_w_out)
    P = {"w_in": f32(rwkv_w_in[0]), "mu": f32(rwkv_mu[0]), "w0": f32(rwkv_w0[0]), "w2": f32(rwkv_w2[0]),
         "a0": f32(rwkv_a0[0]), "a2": f32(rwkv_a2[0]), "g2": f32(rwkv_g2[0]), "k_k": f32(rwkv_k_k[0]),
         "k_a": f32(rwkv_k_a[0]), "r_k": f32(rwkv_r_k[0]), "ln_w": f32(rwkv_ln_w[0]), "ln_b": f32(rwkv_ln_b[0])}
    rw = [host_rwkv_inputs(P, r) for r in range(2)]
    shared["rw_win"] = np.stack([rw[0]["w_in"], rw[1]["w_in"]])
    shared["rw_vecs"] = np.stack([rw[0]["vecs"], rw[1]["vecs"]])
    shared["rw_w2"] = np.stack([rw[0]["w2"], rw[1]["w2"]])
    shared["rw_a2"] = np.stack([rw[0]["a2"], rw[1]["a2"]])
    shared["rw_g2"] = np.stack([rw[0]["g2"], rw[1]["g2"]])
    shared["rw_cst"] = rw[0]["cst"]
    shared["rw_wout"] = f32(rwkv_w_out[0])
    ml = [host_mlstm_inputs(f32(mlstm_w_in[0]), f32(mlstm_conv_w[0]), f32(mlstm_conv_b[0]), f32(mlstm_gate_b[0]),
                            f32(mlstm_norm_g[0]), r) for r in range(2)]
    for nm, key in (("ml_win", "w_in"), ("ml_wgate", "wgate"), ("ml_cw", "cw"), ("ml_cb", "cb"), ("ml_gb", "gb"),
                    ("ml_ng", "ng")):
        shared[nm] = np.stack([ml[0][key], ml[1][key]])
    shared["ml_wout"] = f32(mlstm_w_out[0])
    shared["f_wup"] = f32(ffn_w_up)
    shared["f_wdown"] = f32(ffn_w_down)
    cws, cbs = zip(*[host_conv(f32(ffn_conv_w[i]), f32(ffn_conv_b[i])) for i in range(4)])
    shared["f_cw"] = np.stack(cws)
    shared["f_cb"] = np.stack(cbs)
    maps = []
    for b in range(NFC):
        m = dict(shared)
        m["x_fm"] = np.stack([to_fm(x[b, r * NT:(r + 1) * NT]) for r in range(2)])
        m["c4"] = np.ascontiguousarray(np.broadcast_to(c[b][None], (4, D)))
        maps.append(m)
    res = run(_prog("fused", build_fused_prog), maps)
    out = np.empty((B, T, D), np.float32)
    for b in range(NFC):
        o = np.asarray(res[b]["out_fm"])
        for r in range(2):
            out[b, r * NT:(r + 1) * NT] = from_fm(o[r])
    return out
```
